# Optimizing a Trainium2 kernel written in Bass

```python
import math
import jax, jax.numpy as jnp
from jax import lax
import numpy as np

D_MODEL = 1024
BATCH = 8
SEQ = 4096
DEPTH = 4

GRID_W = 64
NA_HEADS = 8
NA_HEAD_DIM = 64
NA_WIDTH = NA_HEADS * NA_HEAD_DIM
NA_WIN_H = 8
NA_WIN_W = 16
SSD_HEADS = 16
SSD_HEAD_DIM = 64
SSD_WIDTH = SSD_HEADS * SSD_HEAD_DIM
SSD_GROUPS = 2
SSD_HEADS_PER_GROUP = SSD_HEADS // SSD_GROUPS
SSD_STATE = 128
SSD_CONV = 5
SSD_CHUNK = 128
SSD_CONV_DIM = SSD_WIDTH + 2 * SSD_GROUPS * SSD_STATE
FNET_GROUPS = 4
FNET_GROUP_DIM = 128
FNET_WIDTH = FNET_GROUPS * FNET_GROUP_DIM
D_MIX = NA_WIDTH + SSD_WIDTH + FNET_WIDTH
IN_SPLITS = (NA_WIDTH, NA_WIDTH, NA_WIDTH, NA_WIDTH,
             SSD_WIDTH, SSD_CONV_DIM, 2 * SSD_HEADS,
             FNET_WIDTH, FNET_WIDTH)
IN_COLS = sum(IN_SPLITS)
IN_SPLIT_POINTS = tuple(int(v) for v in np.cumsum(IN_SPLITS)[:-1])
RMS_EPS = 1e-6

kernel_name = 'hymba_style_natten_ssd_fnet_encoder'


def rms_norm(x, w):
    xf = x.astype(jnp.float32)
    y = xf * lax.rsqrt(jnp.mean(xf * xf, axis=-1, keepdims=True) + RMS_EPS)
    return (y * w.astype(jnp.float32)).astype(x.dtype)


def neighbourhood_attention(q, k, v, rpb):
    b, s, h, dh = q.shape
    rows = s // GRID_W
    wh = min(NA_WIN_H, rows)
    ww = NA_WIN_W
    qg = q.reshape(b, rows, GRID_W, h, dh)
    kg = k.reshape(b, rows, GRID_W, h, dh)
    vg = v.reshape(b, rows, GRID_W, h, dh)
    row_ids = jnp.arange(rows)
    row_start = jnp.clip(row_ids - wh // 2, 0, rows - wh)
    col_ids = jnp.arange(GRID_W)
    col_start = jnp.clip(col_ids - ww // 2, 0, GRID_W - ww)
    col_idx = col_start[:, None] + jnp.arange(ww)[None, :]
    col_rel = col_idx - col_ids[:, None] + (NA_WIN_W - 1)
    bias_cols = rpb[:, :, col_rel].astype(jnp.float32)
    scale = dh ** -0.5

    def one_row(args):
        q_r, r, r0 = args
        k_band = lax.dynamic_slice_in_dim(kg, r0, wh, axis=1)
        v_band = lax.dynamic_slice_in_dim(vg, r0, wh, axis=1)
        k_win = k_band[:, :, col_idx]
        v_win = v_band[:, :, col_idx]
        row_rel = r0 + jnp.arange(wh) - r + (NA_WIN_H - 1)
        bias = jnp.take(bias_cols, row_rel, axis=1).transpose(0, 2, 1, 3)
        logits = jnp.einsum('bqhd,bwqkhd->bhqwk', q_r, k_win).astype(jnp.float32) * scale + bias[None]
        p = jax.nn.softmax(logits.reshape(b, h, GRID_W, wh * ww), axis=-1)
        p = p.reshape(b, h, GRID_W, wh, ww).astype(v.dtype)
        return jnp.einsum('bhqwk,bwqkhd->bqhd', p, v_win)

    out = lax.map(one_row, (jnp.moveaxis(qg, 1, 0), row_ids, row_start))
    return jnp.moveaxis(out, 0, 1).reshape(b, s, h * dh)


def centred_depthwise_conv(u, w, bias):
    s = u.shape[1]
    pad = SSD_CONV // 2
    up = jnp.pad(u, ((0, 0), (pad, pad), (0, 0)))
    out = bias + up[:, 0:s] * w[0]
    for j in range(1, SSD_CONV):
        out = out + up[:, j:j + s] * w[j]
    return out


def segsum(a):
    l = a.shape[-1]
    cs = jnp.cumsum(a, axis=-1)
    diff = cs[..., :, None] - cs[..., None, :]
    mask = jnp.tril(jnp.ones((l, l), dtype=bool))
    return jnp.where(mask, diff, -jnp.inf)


def ssd_scan(x, dt, a, bm, cm):
    b, s, g, kh, p = x.shape
    n = bm.shape[-1]
    l = SSD_CHUNK
    c = s // l
    xd = (x * dt[..., None]).reshape(b, c, l, g, kh, p)
    adt = (dt * a).reshape(b, c, l, g, kh).transpose(0, 3, 4, 1, 2)
    a_cs = jnp.cumsum(adt, axis=-1)
    bc = bm.reshape(b, c, l, g, n)
    cc = cm.reshape(b, c, l, g, n)
    decay_in = jnp.exp(segsum(adt))
    y_diag = jnp.einsum('bclgn,bcsgn,bgkcls,bcsgkp->bclgkp', cc, bc, decay_in, xd)
    decay_states = jnp.exp(a_cs[..., -1:] - a_cs)
    states = jnp.einsum('bclgn,bgkcl,bclgkp->cbgkpn', bc, decay_states, xd)
    chunk_decay = jnp.moveaxis(jnp.exp(a_cs[..., -1]), -1, 0)

    def step(hst, inp):
        st, dec = inp
        return hst * dec[..., None, None] + st, hst

    h0 = jnp.zeros((b, g, kh, p, n), dtype=states.dtype)
    _, prev = lax.scan(step, h0, (states, chunk_decay))
    y_off = jnp.einsum('bclgn,cbgkpn,bgkcl->bclgkp', cc, prev, jnp.exp(a_cs))
    return (y_diag + y_off).reshape(b, s, g, kh, p)


def ssd_mixer(z, xbc, dt_raw, conv_w, conv_b, dt_bias, a_log, d_skip, norm_w):
    b, s, _ = z.shape
    g, kh, p, n = SSD_GROUPS, SSD_HEADS_PER_GROUP, SSD_HEAD_DIM, SSD_STATE
    xbc = jax.nn.silu(centred_depthwise_conv(xbc, conv_w, conv_b))
    xs, bm, cm = jnp.split(xbc, [SSD_WIDTH, SSD_WIDTH + g * n], axis=-1)
    x = xs.reshape(b, s, g, kh, p)
    bm = bm.reshape(b, s, g, n)
    cm = cm.reshape(b, s, g, n)
    dt = jax.nn.softplus(dt_raw.astype(jnp.float32) + dt_bias.reshape(-1).astype(jnp.float32))
    dt = dt.reshape(b, s, 2, g, kh)
    a = -jnp.exp(a_log.astype(jnp.float32)).reshape(2, g, kh)
    flip = lambda t: jnp.flip(t, axis=1)
    y_f = ssd_scan(x, dt[:, :, 0], a[0], bm, cm)
    y_b = flip(ssd_scan(flip(x), flip(dt[:, :, 1]), a[1], flip(bm), flip(cm)))
    y = y_f + y_b + x * d_skip.reshape(g, kh)[..., None]
    y = y.reshape(b, s, SSD_WIDTH).astype(jnp.float32) * jax.nn.silu(z.astype(jnp.float32))
    yg = y.reshape(b, s, g, SSD_WIDTH // g)
    yg = yg * lax.rsqrt(jnp.mean(yg * yg, axis=-1, keepdims=True) + RMS_EPS)
    y = yg.reshape(b, s, SSD_WIDTH) * norm_w.astype(jnp.float32)
    return y.astype(z.dtype)


def fourier_mixer(u, gate, w_f):
    b, s, _ = u.shape
    ug = u.reshape(b, s, FNET_GROUPS, FNET_GROUP_DIM).astype(jnp.float32)
    mixed = jnp.fft.fft2(ug, axes=(1, 3), norm='ortho').real
    y = jnp.einsum('bsgc,gcd->bsgd', mixed, w_f.astype(jnp.float32)).reshape(b, s, FNET_WIDTH)
    return (y * jax.nn.silu(gate.astype(jnp.float32))).astype(u.dtype)


def hybrid_layer(x, norm_w, w_in, rpb, conv_w, conv_b, dt_bias, a_log, d_skip, ssd_norm_w, w_fourier, w_out):
    b, s, _ = x.shape
    h = rms_norm(x, norm_w)
    proj = jnp.einsum('bsd,de->bse', h, w_in)
    q, k, v, g_na, z, xbc, dt_raw, f_in, g_f = jnp.split(proj, IN_SPLIT_POINTS, axis=-1)
    heads = lambda t: t.reshape(b, s, NA_HEADS, NA_HEAD_DIM)
    na = neighbourhood_attention(heads(q), heads(k), heads(v), rpb) * jax.nn.silu(g_na)
    ssd = ssd_mixer(z, xbc, dt_raw, conv_w, conv_b, dt_bias, a_log, d_skip, ssd_norm_w)
    fn = fourier_mixer(f_in, g_f, w_fourier)
    mixed = jnp.concatenate([na.astype(x.dtype), ssd.astype(x.dtype), fn.astype(x.dtype)], axis=-1)
    return x + jnp.einsum('bse,ed->bsd', mixed, w_out).astype(x.dtype)


def setup_inputs(seed: int = 0) -> dict:
    key = jax.random.key(seed)
    ks = jax.random.split(key, 13)
    f32 = jnp.float32
    nrm = lambda kk, shp: jax.random.normal(kk, shp, f32)
    x = nrm(ks[0], (BATCH, SEQ, D_MODEL))
    norm_w = 1.0 + 0.02 * nrm(ks[1], (DEPTH, D_MODEL))
    w_in = nrm(ks[2], (DEPTH, D_MODEL, IN_COLS)) * D_MODEL ** -0.5
    na_rpb = 0.1 * nrm(ks[3], (DEPTH, NA_HEADS, 2 * NA_WIN_H - 1, 2 * NA_WIN_W - 1))
    conv_w = nrm(ks[4], (DEPTH, SSD_CONV, SSD_CONV_DIM)) * SSD_CONV ** -0.5
    conv_b = 0.02 * nrm(ks[5], (DEPTH, SSD_CONV_DIM))
    dt0 = jnp.exp(jax.random.uniform(ks[6], (DEPTH, 2, SSD_HEADS), f32,
                                     minval=math.log(1e-3), maxval=math.log(1e-1)))
    dt_bias = dt0 + jnp.log(-jnp.expm1(-dt0))
    a_log = jnp.log(jax.random.uniform(ks[7], (DEPTH, 2, SSD_HEADS), f32, minval=1.0, maxval=16.0))
    d_skip = 1.0 + 0.1 * nrm(ks[8], (DEPTH, SSD_HEADS))
    ssd_norm_w = 1.0 + 0.02 * nrm(ks[9], (DEPTH, SSD_WIDTH))
    w_fourier = nrm(ks[10], (DEPTH, FNET_GROUPS, FNET_GROUP_DIM, FNET_GROUP_DIM)) * FNET_GROUP_DIM ** -0.5
    w_out = nrm(ks[11], (DEPTH, D_MIX, D_MODEL)) * D_MIX ** -0.5
    final_norm_w = 1.0 + 0.02 * nrm(ks[12], (D_MODEL,))
    return {'x': x, 'norm_w': norm_w, 'w_in': w_in, 'na_rpb': na_rpb, 'conv_w': conv_w,
            'conv_b': conv_b, 'dt_bias': dt_bias, 'a_log': a_log, 'd_skip': d_skip,
            'ssd_norm_w': ssd_norm_w, 'w_fourier': w_fourier, 'w_out': w_out,
            'final_norm_w': final_norm_w}


def reference(x, norm_w, w_in, na_rpb, conv_w, conv_b, dt_bias, a_log, d_skip, ssd_norm_w,
              w_fourier, w_out, final_norm_w):
    for i in range(DEPTH):
        x = hybrid_layer(x, norm_w[i], w_in[i], na_rpb[i], conv_w[i], conv_b[i], dt_bias[i],
                         a_log[i], d_skip[i], ssd_norm_w[i], w_fourier[i], w_out[i])
    return rms_norm(x, final_norm_w)
```

```python
import numpy as np
import ml_dtypes
import concourse.bass as bass
import concourse.mybir as mybir
from concourse.bass_utils import run_bass_kernel_spmd
from contextlib import ExitStack

F32 = mybir.dt.float32
BF16 = mybir.dt.bfloat16
AF = mybir.ActivationFunctionType
ALU = mybir.AluOpType

S = 4096
D = 1024
NL = 4
INC = 5664
NT = 32
EPS = 1e-6
NDS = 80
BIGF = 1.0e30


class Res:
    __slots__ = ("name", "w", "r", "excl", "multi", "wl")

    def __init__(self, name="", excl=False, multi=False):
        self.name = name
        self.w = None
        self.r = {}
        self.excl = excl
        self.multi = multi
        self.wl = []


class PB:
    ENGS = ("pe", "act", "dve", "pool", "sp")

    def __init__(self, nc, st):
        self.nc = nc
        self.ops = {e: [] for e in self.ENGS}
        self.esem = {e: st.enter_context(nc.semaphore("es_" + e)) for e in ("pe", "act", "dve", "pool")}
        self.ecnt = {e: 0 for e in self.esem}
        self.known = {e: {} for e in self.ENGS}
        self.dsems = [st.enter_context(nc.semaphore("ds%d" % i)) for i in range(NDS)]
        self.dcnt = [0] * NDS
        self.dnext = 0
        self.dnext_p = 0
        self.multis = set()

    def _sem(self, key):
        return self.esem[key] if isinstance(key, str) else self.dsems[key[1]]

    def _need(self, e, key, val, waits):
        if key == e and e == "pe":
            return
        if self.known[e].get(key, 0) >= val:
            return
        self.known[e][key] = val
        waits.append((key, val))

    def _deps(self, e, reads, writes):
        waits = []
        for r in reads:
            if r.w is not None:
                self._need(e, r.w[0], r.w[1], waits)
            for k, v in r.wl:
                self._need(e, k, v, waits)
            if r.excl:
                for k, v in r.r.items():
                    if k != e:
                        self._need(e, k, v, waits)
        for w in writes:
            if w.w is not None and not w.multi:
                self._need(e, w.w[0], w.w[1], waits)
            for k, v in w.r.items():
                self._need(e, k, v, waits)
        return waits

    def _mark(self, key, val, reads, writes):
        for r in reads:
            if r.r.get(key, 0) < val:
                r.r[key] = val
        for w in writes:
            if w.multi:
                w.wl.append((key, val))
                self.multis.add(w)
            w.w = (key, val)
            w.r = {}

    def op(self, e, fn, reads=(), writes=()):
        waits = self._deps(e, reads, writes)
        self.ecnt[e] += 1
        self.ops[e].append((waits, fn, (e, 1)))
        self._mark(e, self.ecnt[e], reads, writes)

    def dma(self, out, in_, reads=(), writes=(), q=None):
        if q is None:
            q = "pool" if "DRAM" in str(out.space) else "sp"
        waits = self._deps(q, reads, writes)
        half = NDS // 2
        if q == "pool":
            i = half + self.dnext_p
            self.dnext_p = (self.dnext_p + 1) % half
        else:
            i = self.dnext
            self.dnext = (self.dnext + 1) % half
        key = ("d", i)
        if self.dcnt[i] > 0:
            self._need(q, key, self.dcnt[i], waits)
        self.dcnt[i] += 16
        self.ops[q].append((waits, (lambda eng, o=out, s=in_: eng.dma_start(out=o, in_=s)), (key, 16)))
        self._mark(key, self.dcnt[i], reads, writes)

    def barrier(self):
        for e in self.ENGS:
            waits = []
            for k in self.esem:
                if k != e and self.ecnt[k] > 0:
                    self._need(e, k, self.ecnt[k], waits)
            for i in range(NDS):
                if self.dcnt[i] > 0:
                    self._need(e, ("d", i), self.dcnt[i], waits)
            if waits:
                self.ops[e].append((waits, None, None))
        for w in self.multis:
            w.wl = []
        self.multis = set()

    def emit(self):
        nc = self.nc
        with nc.Block() as block:
            def run(e):
                def body(eng):
                    for waits, fn, inc in self.ops[e]:
                        for key, val in waits:
                            eng.wait_ge(self._sem(key), val)
                        if fn is not None:
                            fn(eng).then_inc(self._sem(inc[0]), inc[1])
                return body
            block.tensor(run("pe"))
            block.scalar(run("act"))
            block.vector(run("dve"))
            block.gpsimd(run("pool"))
            block.sync(run("sp"))


class Ring:
    def __init__(self, st, nc, name, shape, dtype, n, psum=False):
        self.slots = []
        name = _uniq(name)
        for i in range(n):
            if psum:
                assert int(np.prod(shape[1:])) * (4 if dtype == F32 else 2) == 2048 and shape[0] == 128, shape
                t = st.enter_context(nc.psum_tensor("%s%d" % (name, i), shape, dtype))
            else:
                t = st.enter_context(nc.sbuf_tensor("%s%d" % (name, i), shape, dtype))
            self.slots.append((t, Res("%s%d" % (name, i), excl=psum)))
        self.i = 0

    def next(self):
        s = self.slots[self.i]
        self.i = (self.i + 1) % len(self.slots)
        return s


_UID = [0]


def _uniq(name):
    _UID[0] += 1
    return "%s_u%d" % (name, _UID[0])


def sb(st, nc, name, shape, dtype):
    return st.enter_context(nc.sbuf_tensor(_uniq(name), shape, dtype)), Res(name)


CQ, CK, CV, CG, CZ, CX, CDT, CF, CGF = 0, 512, 1024, 1536, 2048, 3072, 4608, 4640, 5152


def build_program(n_layers=NL, debug=None):
    nc = bass.Bass("TRN2", target_bir_lowering=False)
    dbg = debug or {}

    def din(name, shape, dt=F32):
        return nc.dram_tensor(name, list(shape), dt, kind="ExternalInput").ap()

    def dscr(name, shape, dt):
        kind = "ExternalOutput" if name in dbg else "Internal"
        return nc.dram_tensor(name, list(shape), dt, kind=kind).ap()

    x_in = din("x", [S, D])
    norm_w = din("norm_w", [NL, 128, 8])
    w_in = din("w_in", [NL, D, INC])
    w_out = din("w_out", [NL, 2048, D])
    final_norm_w = din("final_norm_w", [1, D])
    ident_d = din("ident", [128, 128])
    rpbT = din("rpbT", [NL, 8, 64, 15, 64])
    conv_wT = din("conv_wT", [NL, 12, 128, 5])
    conv_bT = din("conv_bT", [NL, 128, 12])
    dt_bias = din("dt_bias", [NL, 32])
    a_log = din("a_log", [NL, 32])
    d_skip = din("d_skip", [NL, 16])
    ssd_norm_w = din("ssd_norm_w", [NL, 1024])
    tri_f_d = din("tri_f", [128, 128])
    tri_b_d = din("tri_b", [128, 128])
    sel_d = din("sel", [128, 32 * 128])
    mrows_d = din("mrows", [32, 128], BF16)
    w_fourier = din("w_fourier", [NL, 4, 128, 128])
    fcs_d = din("fcs", [128, 32, 256], BF16)
    ccs_d = din("ccs", [128, 2, 128])
    w3_d = din("w3", [128, 64], BF16)
    out_d = nc.dram_tensor("out", [S, D], F32, kind="ExternalOutput").ap()

    xres = dscr("xres", [S, D], F32)
    qT = dscr("qT", [4, 128, S], BF16)
    kT = dscr("kT", [4, 128, S], BF16)
    v_tok = dscr("v_tok", [S, 512], BF16)
    gna = dscr("gna", [S, 512], BF16)
    zs = dscr("zs", [S, 1024], BF16)
    xbcT = dscr("xbcT", [12, 128, S], BF16)
    dtraw = dscr("dtraw", [S, 32], F32)
    u_tok = dscr("u_tok", [S, 512], BF16)
    gf = dscr("gf", [S, 512], BF16)
    mixed = dscr("mixed", [S, 2048], BF16)
    x_tok = dscr("x_tok", [S, 1024], BF16)
    b_tok = dscr("b_tok", [S, 256], BF16)
    hb_d = dscr("hb_d", [32, 128, 1024], BF16)
    zbuf = dscr("zbuf", [128, 32, 2, 4, 128], BF16)

    R = {n: Res(n, multi=True) for n in ["xres", "qT", "kT", "v_tok", "gna", "zs", "xbcT", "dtraw", "u_tok", "gf",
                             "mixed", "out", "win", "wout", "x_tok", "b_tok", "hb_d", "zbuf"]}

    with ExitStack() as gst:
        pb = PB(nc, gst)
        ident_f, r_identf = sb(gst, nc, "ident_f", [128, 128], F32)
        ident_b, r_identb = sb(gst, nc, "ident_b", [128, 128], BF16)
        pb.dma(ident_f[:], ident_d[:, :], writes=[r_identf])
        pb.op("dve", lambda e: e.tensor_copy(out=ident_b[:], in_=ident_f[:]), reads=[r_identf], writes=[r_identb])

        nw_all, r_nw = sb(gst, nc, "nw_all", [128, NL, 8], F32)
        for L_ in range(NL):
            pb.dma(nw_all[:, L_, :], norm_w[L_], writes=[r_nw])
        outer_env = dict(locals())

        def do_layer(L):
            xsrc = x_in if L == 0 else xres
            last = (L == n_layers - 1)
            with ExitStack() as st:
                wb, r_wb = sb(st, nc, "p1_w", [128, 8, INC], BF16)
                nw = nw_all[:, L, :]
                wst = Ring(st, nc, "p1_wst", [128, 1416], F32, 3)
                n = 0
                r_wbq = [Res("wbq%d" % i_) for i_ in range(4)]

                def wq(c0, w):
                    return [r_wbq[i_] for i_ in range(c0 // 1416, (c0 + w - 1) // 1416 + 1)]

                for qd in range(4):
                    for kc in range(8):
                        t, r = wst.next()
                        pb.dma(t[:], w_in[L, kc * 128:(kc + 1) * 128, qd * 1416:(qd + 1) * 1416], writes=[r])
                        if n % 2 == 0:
                            pb.op("dve", lambda e, t=t, kc=kc, qd=qd: e.tensor_scalar(
                                out=wb[:, kc, qd * 1416:(qd + 1) * 1416], in0=t[:], scalar1=nw[:, kc:kc + 1],
                                scalar2=None, op0=ALU.mult), reads=[r, r_nw], writes=[r_wbq[qd]])
                        else:
                            pb.op("act", lambda e, t=t, kc=kc, qd=qd: e.activation(
                                out=wb[:, kc, qd * 1416:(qd + 1) * 1416], in_=t[:], func=AF.Copy, scale=nw[:, kc:kc + 1]),
                                reads=[r, r_nw], writes=[r_wbq[qd]])
                        n += 1
                xt_ring = Ring(st, nc, "p1_x", [128, D], F32, 8)
                sq_ring = Ring(st, nc, "p1_sq", [128, D], BF16, 2)
                ss_ring = Ring(st, nc, "p1_ss", [128, 2], F32, 8)
                h_ring = Ring(st, nc, "p1_h", [128, D], BF16, 8)
                hT_ring = Ring(st, nc, "p1_hT", [128, 8, 512], BF16, 2)
                tp_ring = Ring(st, nc, "p1_tp", [128, 8, 128], BF16, 2, psum=True)
                mm_ring = Ring(st, nc, "p1_mm", [128, 512], F32, 5, psum=True)
                fo_ring = Ring(st, nc, "p1_fo", [128, 512], BF16, 4)
                to_ring = Ring(st, nc, "p1_to", [128, 3072], BF16, 3)
                to_res = {i: [Res("to%d_%d" % (i, b_)) for b_ in range(6)] for i in range(3)}
                dt_ring = Ring(st, nc, "p1_dt", [128, 32], F32, 2)
                ev = [0]

                def evac(out_ap, ps, r_ps, r_out, silu=False):
                    if silu:
                        pb.op("act", lambda e: e.activation(out=out_ap, in_=ps, func=AF.Silu), reads=[r_ps], writes=[r_out])
                    else:
                        ev[0] += 1
                        if ev[0] % 2 == 0:
                            pb.op("act", lambda e: e.activation(out=out_ap, in_=ps, func=AF.Copy), reads=[r_ps], writes=[r_out])
                        else:
                            pb.op("dve", lambda e: e.tensor_copy(out=out_ap, in_=ps), reads=[r_ps], writes=[r_out])

                def p1_loads(tb):
                    tiles = []
                    for j in range(4):
                        t0 = tb * 512 + j * 128
                        xt, r_xt = xt_ring.next()
                        pb.dma(xt[:], xsrc[t0:t0 + 128, :], reads=[R["xres"]], writes=[r_xt])
                        tiles.append((xt, r_xt))
                    return tiles

                def p1_stats(tiles):
                    hs = []
                    for j in range(4):
                        xt, r_xt = tiles[j]
                        sq, r_sq = sq_ring.next()
                        ss, r_ss = ss_ring.next()
                        pb.op("act", lambda e, sq=sq, xt=xt, ss=ss: e.activation(
                            out=sq[:], in_=xt[:], func=AF.Square, accum_out=ss[:, 0:1]), reads=[r_xt], writes=[r_sq, r_ss])
                        pb.op("dve", lambda e, ss=ss: e.tensor_scalar(
                            out=ss[:, 1:2], in0=ss[:, 0:1], scalar1=1.0 / D, scalar2=EPS, op0=ALU.mult, op1=ALU.add),
                            reads=[r_ss], writes=[r_ss])
                        pb.op("act", lambda e, ss=ss: e.activation(out=ss[:, 1:2], in_=ss[:, 1:2], func=AF.Sqrt),
                              reads=[r_ss], writes=[r_ss])
                        pb.op("dve", lambda e, ss=ss: e.reciprocal(out=ss[:, 1:2], in_=ss[:, 1:2]),
                              reads=[r_ss], writes=[r_ss])
                        h, r_h = h_ring.next()
                        pb.op("dve", lambda e, h=h, xt=xt, ss=ss: e.tensor_scalar(
                            out=h[:], in0=xt[:], scalar1=ss[:, 1:2], scalar2=None, op0=ALU.mult),
                            reads=[r_xt, r_ss], writes=[r_h])
                        hs.append((h, r_h))
                    return hs

                def p1_tr(hs):
                    hT, r_hT = hT_ring.next()
                    for j in range(4):
                        h, r_h = hs[j]
                        tp, r_tp = tp_ring.next()
                        for kc in range(8):
                            pb.op("pe", lambda e, tp=tp, h=h, kc=kc: e.transpose(
                                out=tp[:, kc, :], in_=h[:, kc * 128:(kc + 1) * 128], identity=ident_b[:]),
                                reads=[r_h, r_identb], writes=[r_tp])
                        pb.op("act", lambda e, hT=hT, tp=tp, j=j: e.activation(
                            out=hT[:, :, j * 128:(j + 1) * 128], in_=tp[:], func=AF.Copy), reads=[r_tp], writes=[r_hT])
                    return hT, r_hT

                nxt = p1_tr(p1_stats(p1_loads(0)))
                nxt_tiles = p1_loads(1)
                for tb in range(8):
                    hT, r_hT = nxt
                    if tb + 1 < 8:
                        nxt_hs = p1_stats(nxt_tiles)
                        if tb + 2 < 8:
                            nxt_tiles = p1_loads(tb + 2)
                    def fm_part(cis, tb=tb, hT=hT, r_hT=r_hT):
                      for ci in cis:
                        if ci < 4:
                            c0, dst, rd = CQ + ci * 128, qT[ci], R["qT"]
                        elif ci < 8:
                            c0, dst, rd = CK + (ci - 4) * 128, kT[ci - 4], R["kT"]
                        else:
                            c0, dst, rd = CX + (ci - 8) * 128, xbcT[ci - 8], R["xbcT"]
                        ps, r_ps = mm_ring.next()
                        for kc in range(8):
                            pb.op("pe", lambda e, ps=ps, kc=kc, c0=c0, hT=hT: e.matmul(
                                ps[:], lhsT=wb[:, kc, c0:c0 + 128], rhs=hT[:, kc, :], start=(kc == 0), stop=(kc == 7)),
                                reads=wq(c0, 128) + [r_hT], writes=[r_ps])
                        fo, r_fo = fo_ring.next()
                        evac(fo[:], ps[:], r_ps, r_fo)
                        pb.dma(dst[:, tb * 512:(tb + 1) * 512], fo[:], reads=[r_fo], writes=[rd])

                    if tb == 0:
                        fm_part(range(8))
                    else:
                        fm_part(range(20))
                    if tb + 1 < 8:
                        nxt = p1_tr(nxt_hs)
                    for j in range(4):
                        t0 = tb * 512 + j * 128
                        to, _r_to_unused = to_ring.next()
                        to_i = to_ring.i
                        blocks = [(CV, 0, False), (CG, 512, True), (CZ, 1024, True), (CZ + 512, 1536, True),
                                  (CF, 2048, False), (CGF, 2560, True)]
                        for bi, (c0, o0, silu) in enumerate(blocks):
                            r_to = to_res[to_i][bi]
                            ps, r_ps = mm_ring.next()
                            for kc in range(8):
                                pb.op("pe", lambda e, ps=ps, kc=kc, c0=c0, hT=hT, j=j: e.matmul(
                                    ps[:], lhsT=hT[:, kc, j * 128:(j + 1) * 128], rhs=wb[:, kc, c0:c0 + 512],
                                    start=(kc == 0), stop=(kc == 7)), reads=wq(c0, 512) + [r_hT], writes=[r_ps])
                            evac(to[:, o0:o0 + 512], ps[:], r_ps, r_to, silu=silu)
                        ps, r_ps = mm_ring.next()
                        for kc in range(8):
                            pb.op("pe", lambda e, ps=ps, kc=kc, hT=hT, j=j: e.matmul(
                                ps[:, 0:32], lhsT=hT[:, kc, j * 128:(j + 1) * 128], rhs=wb[:, kc, CDT:CDT + 32],
                                start=(kc == 0), stop=(kc == 7)), reads=wq(CDT, 32) + [r_hT], writes=[r_ps])
                        dtt, r_dtt = dt_ring.next()
                        pb.op("dve", lambda e, dtt=dtt, ps=ps: e.tensor_copy(out=dtt[:], in_=ps[:, 0:32]),
                              reads=[r_ps], writes=[r_dtt])
                        pb.dma(dtraw[t0:t0 + 128, :], dtt[:], reads=[r_dtt], writes=[R["dtraw"]])
                        rr = to_res[to_i]
                        pb.dma(v_tok[t0:t0 + 128, :], to[:, 0:512], reads=[rr[0]], writes=[R["v_tok"]])
                        pb.dma(gna[t0:t0 + 128, :], to[:, 512:1024], reads=[rr[1]], writes=[R["gna"]])
                        pb.dma(zs[t0:t0 + 128, :], to[:, 1024:2048], reads=[rr[2], rr[3]], writes=[R["zs"]])
                        pb.dma(u_tok[t0:t0 + 128, :], to[:, 2048:2560], reads=[rr[4]], writes=[R["u_tok"]])
                        pb.dma(gf[t0:t0 + 128, :], to[:, 2560:3072], reads=[rr[5]], writes=[R["gf"]])
                    if tb == 0:
                        fm_part(range(8, 20))
            pb.barrier()
            if dbg.get("stop") == "p1":
                return True

            env = dict(outer_env)
            env.update(locals())
            if "skip_na" not in dbg:
                phase_na(nc, pb, L, R, env)
                pb.barrier()
            if "skip_ssd" not in dbg:
                phase_ssd(nc, pb, L, R, env)
                pb.barrier()
            with ExitStack() as stw:
                wo, r_wo = sb(stw, nc, "p5_w", [128, 16, D], BF16)
                wst16 = [sb(stw, nc, "p5_wst", [128, D], F32) for _ in range(16)]
                for kc in range(16):
                    pb.dma(wst16[kc][0][:], w_out[L, kc * 128:(kc + 1) * 128, :], writes=[wst16[kc][1]])
                if "skip_fnet" not in dbg:
                    phase_fnet(nc, pb, L, R, env)
                for kc in range(16):
                    t, r = wst16[kc]
                    if kc % 2 == 0:
                        pb.op("dve", lambda e, t=t, kc=kc: e.tensor_copy(out=wo[:, kc, :], in_=t[:]), reads=[r], writes=[r_wo])
                    else:
                        pb.op("act", lambda e, t=t, kc=kc: e.activation(out=wo[:, kc, :], in_=t[:], func=AF.Copy), reads=[r], writes=[r_wo])
                pb.barrier()
                if dbg.get("stop") == "mix":
                    return True

                with ExitStack() as st:
                    if last:
                        fnw, r_fnw = sb(st, nc, "p5_fnw", [128, D], F32)
                        pb.dma(fnw[:], final_norm_w[0:1, :].broadcast_to([128, D]), writes=[r_fnw])
                    m_ring = Ring(st, nc, "p5_m", [128, 2048], BF16, 3)
                    x_ring = Ring(st, nc, "p5_x", [128, D], F32, 4)
                    mT_ring = Ring(st, nc, "p5_mT", [128, 16, 128], BF16, 3)
                    tp_ring = Ring(st, nc, "p5_tp", [128, 8, 128], BF16, 2, psum=True)
                    mm_ring = Ring(st, nc, "p5_mm", [128, 512], F32, 4, psum=True)
                    xo_ring = Ring(st, nc, "p5_xo", [128, D], F32, 2)
                    sq_ring = Ring(st, nc, "p5_sq", [128, D], BF16, 2)
                    ss_ring = Ring(st, nc, "p5_ss", [128, 2], F32, 2)
                    yo_ring = Ring(st, nc, "p5_yo", [128, D], F32, 2)
                    def p5_prep(t):
                        t0 = t * 128
                        m, r_m = m_ring.next()
                        pb.dma(m[:], mixed[t0:t0 + 128, :], reads=[R["mixed"]], writes=[r_m])
                        xt, r_xt = x_ring.next()
                        pb.dma(xt[:], xsrc[t0:t0 + 128, :], reads=[R["xres"]], writes=[r_xt])
                        mT, r_mT = mT_ring.next()
                        for hh in range(2):
                            tp, r_tp = tp_ring.next()
                            for kc in range(8):
                                kk = hh * 8 + kc
                                pb.op("pe", lambda e, tp=tp, m=m, kc=kc, kk=kk: e.transpose(
                                    out=tp[:, kc, :], in_=m[:, kk * 128:(kk + 1) * 128], identity=ident_b[:]),
                                    reads=[r_m, r_identb], writes=[r_tp])
                            if hh == 0:
                                pb.op("act", lambda e, mT=mT, tp=tp: e.activation(
                                    out=mT[:, 0:8, :], in_=tp[:], func=AF.Copy), reads=[r_tp], writes=[r_mT])
                            else:
                                pb.op("dve", lambda e, mT=mT, tp=tp: e.tensor_copy(out=mT[:, 8:16, :], in_=tp[:]),
                                      reads=[r_tp], writes=[r_mT])
                        return xt, r_xt, mT, r_mT

                    nxt5 = p5_prep(0)
                    for t in range(NT):
                        t0 = t * 128
                        xt, r_xt, mT, r_mT = nxt5
                        xo, r_xo = xo_ring.next()
                        for cb in range(2):
                            if cb == 1 and t + 1 < NT:
                                nxt5 = p5_prep(t + 1)
                            ps, r_ps = mm_ring.next()
                            for kc in range(16):
                                pb.op("pe", lambda e, ps=ps, kc=kc, cb=cb, mT=mT: e.matmul(
                                    ps[:], lhsT=mT[:, kc, :], rhs=wo[:, kc, cb * 512:(cb + 1) * 512],
                                    start=(kc == 0), stop=(kc == 15)), reads=[r_wo, r_mT], writes=[r_ps])
                            pb.op("dve", lambda e, xo=xo, ps=ps, xt=xt, cb=cb: e.tensor_tensor(
                                out=xo[:, cb * 512:(cb + 1) * 512], in0=ps[:], in1=xt[:, cb * 512:(cb + 1) * 512], op=ALU.add),
                                reads=[r_ps, r_xt], writes=[r_xo])
                        if not last:
                            pb.dma(xres[t0:t0 + 128, :], xo[:], reads=[r_xo], writes=[R["xres"]])
                        else:
                            sq, r_sq = sq_ring.next()
                            ss, r_ss = ss_ring.next()
                            pb.op("act", lambda e, sq=sq, xo=xo, ss=ss: e.activation(
                                out=sq[:], in_=xo[:], func=AF.Square, accum_out=ss[:, 0:1]), reads=[r_xo], writes=[r_sq, r_ss])
                            pb.op("dve", lambda e, ss=ss: e.tensor_scalar(
                                out=ss[:, 1:2], in0=ss[:, 0:1], scalar1=1.0 / D, scalar2=EPS, op0=ALU.mult, op1=ALU.add),
                                reads=[r_ss], writes=[r_ss])
                            pb.op("act", lambda e, ss=ss: e.activation(out=ss[:, 1:2], in_=ss[:, 1:2], func=AF.Sqrt),
                                  reads=[r_ss], writes=[r_ss])
                            pb.op("dve", lambda e, ss=ss: e.reciprocal(out=ss[:, 1:2], in_=ss[:, 1:2]),
                                  reads=[r_ss], writes=[r_ss])
                            yo, r_yo = yo_ring.next()
                            pb.op("dve", lambda e, yo=yo, xo=xo, ss=ss: e.scalar_tensor_tensor(
                                out=yo[:], in0=xo[:], scalar=ss[:, 1:2], in1=fnw[:], op0=ALU.mult, op1=ALU.mult),
                                reads=[r_xo, r_ss, r_fnw], writes=[r_yo])
                            pb.dma(out_d[t0:t0 + 128, :], yo[:], reads=[r_yo], writes=[R["out"]])
            pb.barrier()
            return False

        for L in range(n_layers):
            if do_layer(L):
                break
        pb.barrier()
        pb.emit()
    return nc


def phase_na(nc, pb, L, R, env):
    qT, kT, v_tok, gna, mixed, rpbT = (env[k] for k in ["qT", "kT", "v_tok", "gna", "mixed", "rpbT"])
    with ExitStack() as st:
        ets, r_ets = sb(st, nc, "na_ets", [128, 8, 14, 64], BF16)
        tst = Ring(st, nc, "na_tst", [128, 14, 64], F32, 2)
        for h in range(8):
            t, r = tst.next()
            pb.dma(t[0:64], rpbT[L, h, :, 0:14, :], writes=[r])
            pb.dma(t[64:128], rpbT[L, h, :, 1:15, :], writes=[r])
            pb.op("act", lambda e, t=t, h=h: e.activation(out=ets[:, h], in_=t[:], func=AF.Exp), reads=[r], writes=[r_ets])
        kt_ring = Ring(st, nc, "na_kt", [128, S], BF16, 2)
        qc_ring = Ring(st, nc, "na_qc", [128, 64, 128], BF16, 2)
        ve_ring = Ring(st, nc, "na_ve", [128, 32, 2, 65], BF16, 2)
        vo_ring = Ring(st, nc, "na_vo", [128, 32, 2, 65], BF16, 2)
        for (t, r) in qc_ring.slots:
            pb.op("dve", lambda e, t=t: e.memset(t[64:128, :, 0:64], 0.0), writes=[r])
            pb.op("dve", lambda e, t=t: e.memset(t[0:64, :, 64:128], 0.0), writes=[r])
        for (t, r) in ve_ring.slots + vo_ring.slots:
            pb.op("dve", lambda e, t=t: e.memset(t[:, :, :, 64:65], 1.0), writes=[r])
        sp_ring = Ring(st, nc, "na_sp", [128, 4, 2, 64], F32, 4, psum=True)
        op_ring = Ring(st, nc, "na_op", [128, 512], F32, 4, psum=True)
        ex_ring = Ring(st, nc, "na_ex", [128, 4, 2, 64], BF16, 6)
        pt_ring = Ring(st, nc, "na_pt", [128, 4, 2, 64], BF16, 7)
        rc_ring = Ring(st, nc, "na_rc", [64, 16], F32, 2)
        os_ring = Ring(st, nc, "na_os", [64, 8, 130], F32, 2)
        tn_ring = Ring(st, nc, "na_tn", [64, 16, 64], F32, 2)
        g_ring = Ring(st, nc, "na_g", [64, 8, 128], BF16, 2)
        nr_ring = Ring(st, nc, "na_nr", [64, 128], BF16, 4)
        o_ring = Ring(st, nc, "na_o", [64, 8, 128], BF16, 2)
        v4 = v_tok.rearrange("(t p) (h d) -> p t h d", p=128, d=64)
        g3 = gna.rearrange("(r p) c -> p r c", p=64)
        m3 = mixed.rearrange("(r p) c -> p r c", p=64)
        for hp in range(4):
            kt, r_kt = kt_ring.next()
            qc, r_qc = qc_ring.next()
            ve, r_ve = ve_ring.next()
            vo, r_vo = vo_ring.next()
            pb.dma(kt[:], kT[hp], reads=[R["kT"]], writes=[r_kt])
            pb.dma(qc[0:64, :, 0:64], qT[hp, 0:64, :].rearrange("p (r q) -> p r q", q=64), reads=[R["qT"]], writes=[r_qc])
            pb.dma(qc[64:128, :, 64:128], qT[hp, 64:128, :].rearrange("p (r q) -> p r q", q=64), reads=[R["qT"]], writes=[r_qc])
            for tq in range(2):
                for hh in range(2):
                    pb.dma(ve[:, tq * 16:(tq + 1) * 16, hh, 0:64], v4[:, tq * 16:(tq + 1) * 16, 2 * hp + hh, :],
                           reads=[R["v_tok"]], writes=[r_ve])
            vsh = v_tok[64:64 + 31 * 128, :].rearrange("(t p) (h d) -> p t h d", p=128, d=64)
            for tq in range(2):
                t1 = min(31, (tq + 1) * 16)
                for hh in range(2):
                    pb.dma(vo[:, tq * 16:t1, hh, 0:64], vsh[:, tq * 16:t1, 2 * hp + hh, :],
                           reads=[R["v_tok"]], writes=[r_vo])
            def na_stage_a(r):
                r0 = min(max(r - 4, 0), 56)
                sp, r_sp = sp_ring.next()
                for c in range(4):
                    k0 = (r0 + 2 * c) * 64
                    pb.op("pe", lambda e, sp=sp, c=c, k0=k0, r=r, kt=kt, qc=qc: e.matmul(
                        sp[:, c, :, :], lhsT=kt[:, k0:k0 + 128], rhs=qc[:, r, :],
                        start=True, stop=True), reads=[r_kt, r_qc], writes=[r_sp])
                ex, r_ex = ex_ring.next()
                pb.op("act", lambda e, ex=ex, sp=sp: e.activation(out=ex[:], in_=sp[:], func=AF.Exp, scale=0.125),
                      reads=[r_sp], writes=[r_ex])
                pt, r_pt = pt_ring.next()
                m0 = r0 - r + 7
                for hh in range(2):
                    pb.op("dve", lambda e, pt=pt, ex=ex, m0=m0, hp=hp, hh=hh: e.tensor_tensor(
                        out=pt[:, :, hh, :], in0=ex[:, :, hh, :], in1=ets[:, 2 * hp + hh, m0:m0 + 7:2, :], op=ALU.mult),
                        reads=[r_ex, r_ets], writes=[r_pt])
                return pt, r_pt

            def na_stage_b(r, pt, r_pt, gt, r_gt, ot, r_ot, ostg):
                r0 = min(max(r - 4, 0), 56)
                ri = r % 8
                o_bank, r_ops = op_ring.next()
                o_ps = o_bank[0:64, 0:130].rearrange("p (h d) -> p h d", d=65)
                for hh in range(2):
                    for c in range(4):
                        row = r0 + 2 * c
                        if row % 2 == 0:
                            vt, r_vt, ti = ve, r_ve, row // 2
                        else:
                            vt, r_vt, ti = vo, r_vo, (row - 1) // 2
                        pb.op("pe", lambda e, o_ps=o_ps, pt=pt, hh=hh, c=c, vt=vt, ti=ti: e.matmul(
                            o_ps[:, hh, :], lhsT=pt[:, c, hh, :], rhs=vt[:, ti, hh, :],
                            start=(c == 0), stop=(c == 3)), reads=[r_pt, r_vt], writes=[r_ops])
                osg, r_osg = ostg
                pb.op("act", lambda e, osg=osg, o_bank=o_bank, ri=ri: e.activation(
                    out=osg[:, ri, :], in_=o_bank[0:64, 0:130], func=AF.Copy), reads=[r_ops], writes=[r_osg])
                if ri == 7:
                    rc, r_rc = rc_ring.next()
                    ov = osg[:].rearrange("p r (h d) -> p (r h) d", d=65)
                    pb.op("dve", lambda e, rc=rc, ov=ov: e.reciprocal(out=rc[:], in_=ov[:, :, 64]),
                          reads=[r_osg], writes=[r_rc])
                    tn, r_tn = tn_ring.next()
                    pb.op("dve", lambda e, tn=tn, ov=ov, rc=rc: e.tensor_tensor(
                        out=tn[:], in0=ov[:, :, 0:64], in1=bc_last(rc[:], 64), op=ALU.mult),
                        reads=[r_osg, r_rc], writes=[r_tn])
                    pb.op("dve", lambda e, ot=ot, tn=tn, gt=gt: e.tensor_tensor(
                        out=ot[:].rearrange("p r c -> p (r c)"), in0=tn[:].rearrange("p a d -> p (a d)"),
                        in1=gt[:].rearrange("p r c -> p (r c)"), op=ALU.mult), reads=[r_tn, r_gt], writes=[r_ot])

            AHEAD = 4
            pend = [na_stage_a(r) for r in range(AHEAD)]
            for rb in range(8):
                gt, r_gt = g_ring.next()
                pb.dma(gt[:], g3[:, rb * 8:(rb + 1) * 8, hp * 128:(hp + 1) * 128], reads=[R["gna"]], writes=[r_gt])
                ot, r_ot = o_ring.next()
                ostg = os_ring.next()
                for ri in range(8):
                    r = rb * 8 + ri
                    if r + AHEAD < 64:
                        pend.append(na_stage_a(r + AHEAD))
                    pt, r_pt = pend.pop(0)
                    na_stage_b(r, pt, r_pt, gt, r_gt, ot, r_ot, ostg)
                pb.dma(m3[:, rb * 8:(rb + 1) * 8, hp * 128:(hp + 1) * 128], ot[:], reads=[r_ot], writes=[R["mixed"]])


def bc_last(ap, n):
    shp = list(ap.shape)
    return ap.unsqueeze(len(shp)).broadcast_to(shp + [n])


def phase_ssd(nc, pb, L, R, env):
    g_ = lambda k: env[k]
    xbcT, dtraw, zs, mixed = g_("xbcT"), g_("dtraw"), g_("zs"), g_("mixed")
    x_tok, b_tok, hb_d = g_("x_tok"), g_("b_tok"), g_("hb_d")
    conv_wT, conv_bT, dt_bias, a_log, d_skip, ssd_norm_w = (g_(k) for k in
        ["conv_wT", "conv_bT", "dt_bias", "a_log", "d_skip", "ssd_norm_w"])
    tri_f_d, tri_b_d, sel_d, mrows_d = g_("tri_f_d"), g_("tri_b_d"), g_("sel_d"), g_("mrows_d")
    ident_f, r_identf, ident_b, r_identb = g_("ident_f"), g_("r_identf"), g_("ident_b"), g_("r_identb")
    with ExitStack() as st:
        bct, r_bct = sb(st, nc, "ss_bct", [128, 4, S], BF16)
        cs3T, r_cs3T = sb(st, nc, "ss_cs3T", [96, S], BF16)
        ncs3T, r_ncs3T = sb(st, nc, "ss_ncs3T", [128, S], BF16)
        sel, r_sel = sb(st, nc, "ss_sel", [128, 32, 128], BF16)
        pb.dma(ncs3T[96:128, :].rearrange("p (c l) -> p c l", l=128),
               mrows_d[:, :].unsqueeze(1).broadcast_to([32, 32, 128]), writes=[r_ncs3T])
        wst, r_wst = sb(st, nc, "ss_wst", [128, 32, 32], F32)
        ef, r_ef = sb(st, nc, "ss_ef", [128, 32, 32], F32)
        cdb, r_cdb = sb(st, nc, "ss_cdb", [128, 32, 32], F32)
        trif, r_trif = sb(st, nc, "ss_trif", [128, 128], F32)
        trib, r_trib = sb(st, nc, "ss_trib", [128, 128], F32)
        di, r_di = sb(st, nc, "ss_di", [128, 16, 128], BF16)
        nwb, r_nwb = sb(st, nc, "ss_nwb", [128, 1024], F32)
        dsk, r_dsk = sb(st, nc, "ss_dsk", [128, 16], F32)
        pb.dma(trif[:], tri_f_d[:, :], writes=[r_trif])
        pb.dma(trib[:], tri_b_d[:, :], writes=[r_trib])
        pb.dma(nwb[:], ssd_norm_w[L:L + 1, :].broadcast_to([128, 1024]), writes=[r_nwb])
        pb.dma(dsk[:], d_skip[L:L + 1, :].broadcast_to([128, 16]), writes=[r_dsk])

        for i in range(16):
            pb.op("dve", lambda e, i=i: e.tensor_scalar(out=di[:, i, :], in0=ident_f[:], scalar1=dsk[:, i:i + 1],
                                                        scalar2=None, op0=ALU.mult), reads=[r_identf, r_dsk], writes=[r_di])
        with ExitStack() as s1:
            selst, r_selst = sb(s1, nc, "ss_selst", [128, 32 * 128], F32)
            pb.dma(selst[:], sel_d[:, :], writes=[r_selst])
            pb.op("dve", lambda e: e.tensor_copy(out=sel[:].rearrange("p a b -> p (a b)"), in_=selst[:]),
                  reads=[r_selst], writes=[r_sel])
            cw, r_cw = sb(s1, nc, "ss_cw", [128, 12, 5], F32)
            cb, r_cb = sb(s1, nc, "ss_cb", [128, 12], F32)
            pb.dma(cw[:], conv_wT[L].rearrange("c p j -> p c j"), writes=[r_cw])
            pb.dma(cb[:], conv_bT[L], writes=[r_cb])
            xp_ring = Ring(s1, nc, "ss_xp", [128, S + 4], BF16, 2)
            for (t, r) in xp_ring.slots:
                pb.op("dve", lambda e, t=t: e.memset(t[:, 0:2], 0.0), writes=[r])
                pb.op("dve", lambda e, t=t: e.memset(t[:, S + 2:S + 4], 0.0), writes=[r])
            dg_ring = Ring(s1, nc, "ss_dg", [128, 5, 128], BF16, 2)
            cps_ring = Ring(s1, nc, "ss_cps", [128, 512], F32, 4, psum=True)
            tps_ring = Ring(s1, nc, "ss_tps", [128, 8, 128], BF16, 3, psum=True)
            xc_ring = Ring(s1, nc, "ss_xc", [128, 512], BF16, 4)
            acc_ring = Ring(s1, nc, "ss_acc", [128, 512], F32, 3)
            xt_ring = Ring(s1, nc, "ss_xt", [128, 4, 128], BF16, 6)
            x3 = x_tok.rearrange("(t p) c -> p t c", p=128)
            b3 = b_tok.rearrange("(t p) c -> p t c", p=128)
            conv_pend = []

            def conv_tr(cc, blk, src, r_src):
                tp, r_tp = tps_ring.next()
                for i in range(4):
                    pb.op("pe", lambda e, tp=tp, src=src, i=i: e.transpose(
                        out=tp[:, i, :], in_=src[:, i * 128:(i + 1) * 128], identity=ident_b[:]),
                        reads=[r_src, r_identb], writes=[r_tp])
                xt, r_xt = xt_ring.next()
                pb.op("dve", lambda e, xt=xt, tp=tp: e.tensor_copy(out=xt[:], in_=tp[:, 0:4, :]),
                      reads=[r_tp], writes=[r_xt])
                if cc < 8:
                    pb.dma(x3[:, blk * 4:(blk + 1) * 4, cc * 128:(cc + 1) * 128], xt[:], reads=[r_xt], writes=[R["x_tok"]], q="sp")
                else:
                    pb.dma(b3[:, blk * 4:(blk + 1) * 4, (cc - 8) * 128:(cc - 7) * 128], xt[:], reads=[r_xt], writes=[R["b_tok"]], q="sp")

            for cc in range(12):
                xp, r_xp = xp_ring.next()
                pb.dma(xp[:, 2:S + 2], xbcT[cc], reads=[R["xbcT"]], writes=[r_xp])
                dg, r_dg = dg_ring.next()
                for j in range(5):
                    pb.op("dve", lambda e, dg=dg, j=j, cc=cc: e.tensor_scalar(
                        out=dg[:, j, :], in0=ident_f[:], scalar1=cw[:, cc, j:j + 1], scalar2=None, op0=ALU.mult),
                        reads=[r_identf, r_cw], writes=[r_dg])
                on_dve = False
                for blk in range(8):
                    if on_dve:
                        ps, r_ps = acc_ring.next()
                        c0_ = blk * 512
                        pb.op("dve", lambda e, ps=ps, xp=xp, cc=cc, c0_=c0_: e.tensor_scalar(
                            out=ps[:], in0=xp[:, c0_:c0_ + 512], scalar1=cw[:, cc, 0:1], scalar2=cb[:, cc:cc + 1],
                            op0=ALU.mult, op1=ALU.add), reads=[r_xp, r_cw, r_cb], writes=[r_ps])
                        for j in range(1, 5):
                            pb.op("dve", lambda e, ps=ps, xp=xp, cc=cc, c0_=c0_, j=j: e.scalar_tensor_tensor(
                                out=ps[:], in0=xp[:, c0_ + j:c0_ + j + 512], scalar=cw[:, cc, j:j + 1], in1=ps[:],
                                op0=ALU.mult, op1=ALU.add), reads=[r_xp, r_cw, r_ps], writes=[r_ps])
                        sbias = 0.0
                    else:
                        ps, r_ps = cps_ring.next()
                        for j in range(5):
                            pb.op("pe", lambda e, ps=ps, dg=dg, xp=xp, j=j, blk=blk: e.matmul(
                                ps[:], lhsT=dg[:, j, :], rhs=xp[:, blk * 512 + j:blk * 512 + j + 512],
                                start=(j == 0), stop=(j == 4)), reads=[r_dg, r_xp], writes=[r_ps])
                        sbias = cb[:, cc:cc + 1]
                    if cc >= 8:
                        dst = bct[:, cc - 8, blk * 512:(blk + 1) * 512]
                        pb.op("act", lambda e, dst=dst, ps=ps, sbias=sbias: e.activation(
                            out=dst, in_=ps[:], func=AF.Silu, bias=sbias), reads=[r_ps, r_cb], writes=[r_bct])
                        src, r_src = dst, r_bct
                    else:
                        xc, r_xc = xc_ring.next()
                        pb.op("act", lambda e, xc=xc, ps=ps, sbias=sbias: e.activation(
                            out=xc[:], in_=ps[:], func=AF.Silu, bias=sbias), reads=[r_ps, r_cb], writes=[r_xc])
                        src, r_src = xc[:], r_xc
                    if cc < 10:
                        conv_pend.append((cc, blk, src, r_src))
                    if len(conv_pend) > 1 or (cc == 11 and blk == 7 and conv_pend) or (cc >= 10 and conv_pend):
                        conv_tr(*conv_pend.pop(0))
        pb.barrier()
        if env['dbg'].get('ssd_stop') == 1:
            return
        with ExitStack() as s2:
            def t32(name):
                return sb(s2, nc, name, [128, 32, 32], F32)
            dtr, r_dtr = t32("ss_dtr")
            dt, r_dt = t32("ss_dt")
            adt, r_adt = t32("ss_adt")
            lnd, r_lnd = t32("ss_lnd")
            cs, r_cs = t32("ss_cs")
            ncs, r_ncs = t32("ss_ncs")
            tot, r_tot = t32("ss_tot")
            tmp, r_tmp = t32("ss_tmp")
            cs3, r_cs3 = sb(s2, nc, "ss_cs3", [128, 32, 96], BF16)
            ncs3, r_ncs3 = sb(s2, nc, "ss_ncs3", [128, 32, 96], BF16)
            dtb, r_dtb = sb(s2, nc, "ss_dtb", [128, 32], F32)
            alb, r_alb = sb(s2, nc, "ss_alb", [128, 32], F32)
            ones, r_ones = sb(s2, nc, "ss_ones", [128, 128], F32)
            ps_ring = Ring(s2, nc, "ss_pps", [128, 512], F32, 3, psum=True)
            tp_ring = Ring(s2, nc, "ss_ptp", [128, 8, 128], BF16, 2, psum=True)
            pb.op("dve", lambda e: e.memset(ones[:], 1.0), writes=[r_ones])
            d3 = dtraw.rearrange("(t p) h -> p t h", p=128)
            pb.dma(dtr[:, 0:16, :], d3[:, 0:16, :], reads=[R["dtraw"]], writes=[r_dtr])
            pb.dma(dtr[:, 16:32, :], d3[:, 16:32, :], reads=[R["dtraw"]], writes=[r_dtr])
            pb.dma(dtb[:], dt_bias[L:L + 1, :].broadcast_to([128, 32]), writes=[r_dtb])
            pb.dma(alb[:], a_log[L:L + 1, :].broadcast_to([128, 32]), writes=[r_alb])
            mid = lambda t: t[:].unsqueeze(1).broadcast_to([128, 32, 32])
            pb.op("act", lambda e: e.activation(out=alb[:], in_=alb[:], func=AF.Exp), reads=[r_alb], writes=[r_alb])
            pb.op("dve", lambda e: e.tensor_scalar(out=alb[:], in0=alb[:], scalar1=-1.0, scalar2=None, op0=ALU.mult),
                  reads=[r_alb], writes=[r_alb])
            pb.op("dve", lambda e: e.tensor_tensor(out=dtr[:], in0=dtr[:], in1=mid(dtb), op=ALU.add),
                  reads=[r_dtr, r_dtb], writes=[r_dtr])
            pb.op("act", lambda e: e.activation(out=tmp[:], in_=dtr[:], func=AF.Exp), reads=[r_dtr], writes=[r_tmp])
            pb.op("act", lambda e: e.activation(out=dt[:], in_=tmp[:], func=AF.Ln, bias=1.0), reads=[r_tmp], writes=[r_dt])
            pb.op("dve", lambda e: e.tensor_tensor(out=adt[:], in0=dt[:], in1=mid(alb), op=ALU.mult),
                  reads=[r_dt, r_alb], writes=[r_adt])
            pb.op("dve", lambda e: e.tensor_scalar(out=tmp[:], in0=dt[:], scalar1=1e-30, scalar2=None, op0=ALU.max),
                  reads=[r_dt], writes=[r_tmp])
            pb.op("act", lambda e: e.activation(out=lnd[:], in_=tmp[:], func=AF.Ln), reads=[r_tmp], writes=[r_lnd])
            for d_, tri, r_tri in ((0, trif, r_trif), (1, trib, r_trib)):
                ps, r_ps = ps_ring.next()
                pv = ps[:].rearrange("p (t h) -> p t h", h=16)
                pb.op("pe", lambda e, pv=pv, tri=tri, d_=d_: e.matmul(pv, lhsT=tri[:], rhs=adt[:, :, d_ * 16:(d_ + 1) * 16],
                                                                      start=True, stop=True), reads=[r_tri, r_adt], writes=[r_ps])
                pb.op("dve", lambda e, pv=pv, d_=d_: e.tensor_copy(out=cs[:, :, d_ * 16:(d_ + 1) * 16], in_=pv),
                      reads=[r_ps], writes=[r_cs])
            for hlf in range(2):
                ps, r_ps = ps_ring.next()
                pv = ps[:].rearrange("p (t h) -> p t h", h=32)
                pb.op("pe", lambda e, pv=pv, hlf=hlf: e.matmul(pv, lhsT=ones[:], rhs=adt[:, hlf * 16:(hlf + 1) * 16, :],
                                                               start=True, stop=True), reads=[r_ones, r_adt], writes=[r_ps])
                pb.op("dve", lambda e, pv=pv, hlf=hlf: e.tensor_copy(out=tot[:, hlf * 16:(hlf + 1) * 16, :], in_=pv),
                      reads=[r_ps], writes=[r_tot])
            pb.op("dve", lambda e: e.tensor_tensor(out=ncs[:], in0=lnd[:], in1=cs[:], op=ALU.subtract),
                  reads=[r_lnd, r_cs], writes=[r_ncs])
            pb.op("dve", lambda e: e.tensor_tensor(out=tmp[:], in0=tot[:], in1=ncs[:], op=ALU.add),
                  reads=[r_tot, r_ncs], writes=[r_tmp])
            pb.op("act", lambda e: e.activation(out=wst[:], in_=tmp[:], func=AF.Exp), reads=[r_tmp], writes=[r_wst])
            pb.op("act", lambda e: e.activation(out=ef[:], in_=cs[:], func=AF.Exp), reads=[r_cs], writes=[r_ef])
            pb.op("act", lambda e: e.activation(out=cdb[:], in_=tot[:], func=AF.Exp), reads=[r_tot], writes=[r_cdb])
            for src, r_src, dst, r_dst in ((cs, r_cs, cs3, r_cs3), (ncs, r_ncs, ncs3, r_ncs3)):
                pb.op("dve", lambda e, src=src, dst=dst: e.tensor_copy(out=dst[:, :, 0:32], in_=src[:]),
                      reads=[r_src], writes=[r_dst])
                pb.op("dve", lambda e, src=src, dst=dst: e.tensor_tensor(out=tmp[:], in0=src[:], in1=dst[:, :, 0:32], op=ALU.subtract),
                      reads=[r_src, r_dst], writes=[r_tmp])
                pb.op("dve", lambda e, dst=dst: e.tensor_copy(out=dst[:, :, 32:64], in_=tmp[:]),
                      reads=[r_tmp], writes=[r_dst])
                pb.op("dve", lambda e, dst=dst: e.tensor_tensor(out=tmp[:], in0=tmp[:], in1=dst[:, :, 32:64], op=ALU.subtract),
                      reads=[r_tmp, r_dst], writes=[r_tmp])
                pb.op("dve", lambda e, dst=dst: e.tensor_copy(out=dst[:, :, 64:96], in_=tmp[:]),
                      reads=[r_tmp], writes=[r_dst])
            for src, r_src, dstT, r_dstT in ((cs3, r_cs3, cs3T, r_cs3T), (ncs3, r_ncs3, ncs3T, r_ncs3T)):
                for t8 in range(4):
                    tp, r_tp = tp_ring.next()
                    for i in range(8):
                        t = t8 * 8 + i
                        pb.op("pe", lambda e, tp=tp, src=src, t=t, i=i: e.transpose(
                            out=tp[0:96, i, :], in_=src[:, t, :], identity=ident_b[:]), reads=[r_src, r_identb], writes=[r_tp])
                    pb.op("act", lambda e, dstT=dstT, tp=tp, t8=t8: e.activation(
                        out=dstT[0:96, t8 * 1024:(t8 + 1) * 1024].rearrange("p (a b) -> p a b", b=128), in_=tp[0:96, :, :],
                        func=AF.Copy), reads=[r_tp], writes=[r_dstT])
        pb.barrier()
        if env['dbg'].get('ssd_stop') == 2:
            return
        with ExitStack() as s3:
            x_ring = Ring(s3, nc, "ss_ax", [128, 1024], BF16, 2)
            b_ring = Ring(s3, nc, "ss_ab", [128, 256], BF16, 2)
            xw_ring = Ring(s3, nc, "ss_axw", [128, 512], BF16, 2)
            hbf_ring = Ring(s3, nc, "ss_ahbf", [128, 2, 512], BF16, 2)
            hb, r_hb = sb(s3, nc, "ss_ahb", [128, 2, 512], F32)
            sps_ring = Ring(s3, nc, "ss_asps", [128, 512], F32, 2, psum=True)
            pb.op("dve", lambda e: e.memset(hb[:], 0.0), writes=[r_hb])
            for c in range(31, -1, -1):
                hbf, r_hbf = hbf_ring.next()
                pb.op("act", lambda e, hbf=hbf: e.activation(out=hbf[:], in_=hb[:], func=AF.Copy), reads=[r_hb], writes=[r_hbf])
                pb.dma(hb_d[c], hbf[:].rearrange("p g f -> p (g f)"), reads=[r_hbf], writes=[R["hb_d"]])
                if c == 0:
                    break
                xt, r_xt = x_ring.next()
                bt, r_bt = b_ring.next()
                pb.dma(xt[:], x_tok[c * 128:(c + 1) * 128, :], reads=[R["x_tok"]], writes=[r_xt])
                pb.dma(bt[:], b_tok[c * 128:(c + 1) * 128, :], reads=[R["b_tok"]], writes=[r_bt])
                for g in range(2):
                    xw, r_xw = xw_ring.next()
                    hs = 16 + g * 8
                    pb.op("dve", lambda e, xw=xw, xt=xt, g=g, c=c, hs=hs: e.tensor_tensor(
                        out=xw[:].rearrange("p (k d) -> p k d", d=64), in0=xt[:, g * 512:(g + 1) * 512].rearrange("p (k d) -> p k d", d=64),
                        in1=bc_last(wst[:, c, hs:hs + 8], 64), op=ALU.mult), reads=[r_xt, r_wst], writes=[r_xw])
                    ps, r_ps = sps_ring.next()
                    pb.op("pe", lambda e, ps=ps, bt=bt, xw=xw, g=g: e.matmul(
                        ps[:], lhsT=bt[:, g * 128:(g + 1) * 128], rhs=xw[:], start=True, stop=True),
                        reads=[r_bt, r_xw], writes=[r_ps])
                    pb.op("dve", lambda e, g=g, c=c, hs=hs: e.tensor_tensor(
                        out=hb[:, g, :].rearrange("p (k d) -> p k d", d=64), in0=hb[:, g, :].rearrange("p (k d) -> p k d", d=64),
                        in1=bc_last(cdb[:, c, hs:hs + 8], 64), op=ALU.mult), reads=[r_hb, r_cdb], writes=[r_hb])
                    pb.op("dve", lambda e, g=g, ps=ps: e.tensor_tensor(out=hb[:, g, :], in0=hb[:, g, :], in1=ps[:], op=ALU.add),
                          reads=[r_hb, r_ps], writes=[r_hb])
        pb.barrier()
        if env['dbg'].get('ssd_stop') == 3:
            return
        with ExitStack() as s4:
            x_ring = Ring(s4, nc, "ss_bx", [128, 1024], BF16, 3)
            b_ring = Ring(s4, nc, "ss_bb", [128, 256], BF16, 3)
            z_ring = Ring(s4, nc, "ss_bz", [128, 1024], BF16, 3)
            h_ring = Ring(s4, nc, "ss_bh", [128, 1024], BF16, 3)
            gm_ring = Ring(s4, nc, "ss_gm", [128, 2, 2, 128], BF16, 2)
            lm_ring = Ring(s4, nc, "ss_lm", [128, 4, 128], BF16, 3)
            mt_ring = Ring(s4, nc, "ss_mt", [128, 32, 128], BF16, 2)
            xw_ring = Ring(s4, nc, "ss_bxw", [128, 512], BF16, 4)
            t1_ring = Ring(s4, nc, "ss_t1", [128, 512], BF16, 4)
            t2_ring = Ring(s4, nc, "ss_t2", [128, 512], BF16, 4)
            yz_ring = Ring(s4, nc, "ss_yz", [128, 512], F32, 4)
            sq_ring = Ring(s4, nc, "ss_sq", [128, 512], BF16, 3)
            ss_ring = Ring(s4, nc, "ss_ss", [128, 2], F32, 6)
            o_ring = Ring(s4, nc, "ss_o", [128, 1024], BF16, 2)
            hf, r_hf = sb(s4, nc, "ss_hf", [128, 2, 512], F32)
            hfb, r_hfb = sb(s4, nc, "ss_hfb", [128, 2, 512], BF16)
            gps_ring = Ring(s4, nc, "ss_gps", [128, 512], F32, 1, psum=True)
            dps_ring = Ring(s4, nc, "ss_dps", [128, 512], F32, 2, psum=True)
            yd_ring = Ring(s4, nc, "ss_yd", [128, 512], F32, 2, psum=True)
            yo_ring = Ring(s4, nc, "ss_yo", [128, 512], F32, 1, psum=True)
            sps_ring = Ring(s4, nc, "ss_bsps", [128, 512], F32, 2, psum=True)
            pb.op("dve", lambda e: e.memset(hf[:], 0.0), writes=[r_hf])
            pb.op("dve", lambda e: e.memset(hfb[:], 0.0), writes=[r_hfb])
            def sw_load(c):
                cs_ = slice(c * 128, (c + 1) * 128)
                xt, r_xt = x_ring.next()
                bt, r_bt = b_ring.next()
                zt, r_zt = z_ring.next()
                ht, r_ht = h_ring.next()
                pb.dma(xt[:], x_tok[cs_, :], reads=[R["x_tok"]], writes=[r_xt])
                pb.dma(bt[:], b_tok[cs_, :], reads=[R["b_tok"]], writes=[r_bt])
                pb.dma(zt[:], zs[cs_, :], reads=[R["zs"]], writes=[r_zt])
                pb.dma(ht[:], hb_d[c], reads=[R["hb_d"]], writes=[r_ht])
                return dict(xt=xt, r_xt=r_xt, bt=bt, r_bt=r_bt, zt=zt, r_zt=r_zt, ht=ht, r_ht=r_ht)

            def sw_dphase(c):
                cs_ = slice(c * 128, (c + 1) * 128)
                gps, r_gps = gps_ring.next()
                gm, r_gm = gm_ring.next()
                for g in range(2):
                    pb.op("pe", lambda e, gps=gps, g=g, cs_=cs_: e.matmul(
                        gps[:, g * 128:(g + 1) * 128], lhsT=bct[:, g, cs_], rhs=bct[:, 2 + g, cs_], start=True, stop=True),
                        reads=[r_bct], writes=[r_gps])
                for g in range(2):
                    pb.op("dve", lambda e, gm=gm, gps=gps, g=g: e.tensor_tensor(
                        out=gm[:, 0, g, :], in0=gps[:, g * 128:(g + 1) * 128], in1=trif[:], op=ALU.mult),
                        reads=[r_gps, r_trif], writes=[r_gm])
                    pb.op("dve", lambda e, gm=gm, gps=gps, g=g: e.tensor_tensor(
                        out=gm[:, 1, g, :], in0=gps[:, g * 128:(g + 1) * 128], in1=trib[:], op=ALU.mult),
                        reads=[r_gps, r_trib], writes=[r_gm])
                mt, r_mt = mt_ring.next()
                for q in range(8):
                    d_, g = q // 4, (q // 2) % 2
                    dps, r_dps = dps_ring.next()
                    dv = dps[:].rearrange("p (j l) -> p j l", l=128)
                    pb.op("pe", lambda e, dv=dv, q=q, cs_=cs_: e.matmul(
                        dv, lhsT=ncs3T[:, cs_], rhs=sel[:, 4 * q:4 * q + 4, :], start=True, stop=False),
                        reads=[r_sel, r_ncs3T], writes=[r_dps])
                    for j in range(4):
                        hd = 4 * q + j
                        pb.op("pe", lambda e, dv=dv, j=j, hd=hd, cs_=cs_: e.matmul(
                            dv[:, j, :], lhsT=sel[0:96, hd, :], rhs=cs3T[:, cs_], start=False, stop=(j == 3)),
                            reads=[r_sel, r_cs3T], writes=[r_dps])
                    lm, r_lm = lm_ring.next()
                    pb.op("act", lambda e, lm=lm, dv=dv: e.activation(out=lm[:], in_=dv, func=AF.Exp), reads=[r_dps], writes=[r_lm])
                    pb.op("dve", lambda e, mt=mt, lm=lm, gm=gm, q=q, d_=d_, g=g: e.scalar_tensor_tensor(
                        out=mt[:, 4 * q:4 * q + 4, :], in0=lm[:], scalar=BIGF,
                        in1=gm[:, d_, g, :].unsqueeze(1).broadcast_to([128, 4, 128]), op0=ALU.min, op1=ALU.mult),
                        reads=[r_lm, r_gm], writes=[r_mt])
                return mt, r_mt

            def sw_ystate(c, ld, mt, r_mt):
                cs_ = slice(c * 128, (c + 1) * 128)
                xt, r_xt, zt, r_zt, ht, r_ht = ld["xt"], ld["r_xt"], ld["zt"], ld["r_zt"], ld["ht"], ld["r_ht"]
                bt, r_bt = ld["bt"], ld["r_bt"]
                o, r_o = o_ring.next()
                tt = []
                for g in range(2):
                    yf, r_yf = yo_ring.next()
                    pb.op("pe", lambda e, yf=yf, g=g, cs_=cs_: e.matmul(
                        yf[:], lhsT=bct[:, 2 + g, cs_], rhs=hfb[:, g, :], start=True, stop=True), reads=[r_bct, r_hfb], writes=[r_yf])
                    t1, r_t1 = t1_ring.next()
                    pb.op("dve", lambda e, t1=t1, yf=yf, g=g, c=c: e.tensor_tensor(
                        out=t1[:].rearrange("p (k d) -> p k d", d=64), in0=yf[:].rearrange("p (k d) -> p k d", d=64),
                        in1=bc_last(ef[:, c, g * 8:g * 8 + 8], 64), op=ALU.mult), reads=[r_yf, r_ef], writes=[r_t1])
                    yb, r_yb = yo_ring.next()
                    pb.op("pe", lambda e, yb=yb, g=g, cs_=cs_, ht=ht: e.matmul(
                        yb[:], lhsT=bct[:, 2 + g, cs_], rhs=ht[:, g * 512:(g + 1) * 512], start=True, stop=True),
                        reads=[r_bct, r_ht], writes=[r_yb])
                    t2, r_t2 = t2_ring.next()
                    pb.op("dve", lambda e, t2=t2, yb=yb, g=g, c=c: e.tensor_tensor(
                        out=t2[:].rearrange("p (k d) -> p k d", d=64), in0=yb[:].rearrange("p (k d) -> p k d", d=64),
                        in1=bc_last(ef[:, c, 16 + g * 8:16 + g * 8 + 8], 64), op=ALU.mult), reads=[r_yb, r_ef], writes=[r_t2])
                    tt.append((t1, r_t1, t2, r_t2))
                yds = []
                for g in range(2):
                    t1, r_t1, t2, r_t2 = tt[g]
                    yd, r_yd = yd_ring.next()
                    ydv = yd[:].rearrange("p (k d) -> p k d", d=64)
                    pb.op("pe", lambda e, yd=yd, t1=t1: e.matmul(yd[:], lhsT=ident_b[:], rhs=t1[:], start=True, stop=False),
                          reads=[r_identb, r_t1], writes=[r_yd])
                    pb.op("pe", lambda e, yd=yd, t2=t2: e.matmul(yd[:], lhsT=ident_b[:], rhs=t2[:], start=False, stop=False),
                          reads=[r_identb, r_t2], writes=[r_yd])
                    for k in range(8):
                        hi = g * 8 + k
                        xs_ = xt[:, hi * 64:(hi + 1) * 64]
                        pb.op("pe", lambda e, ydv=ydv, k=k, hi=hi, xs_=xs_, mt=mt: e.matmul(
                            ydv[:, k, :], lhsT=mt[:, hi, :], rhs=xs_, start=False, stop=False), reads=[r_mt, r_xt], writes=[r_yd])
                        pb.op("pe", lambda e, ydv=ydv, k=k, hi=hi, xs_=xs_, mt=mt: e.matmul(
                            ydv[:, k, :], lhsT=mt[:, 16 + hi, :], rhs=xs_, start=False, stop=False), reads=[r_mt, r_xt], writes=[r_yd])
                        pb.op("pe", lambda e, ydv=ydv, k=k, hi=hi, xs_=xs_: e.matmul(
                            ydv[:, k, :], lhsT=di[:, hi, :], rhs=xs_, start=False, stop=(k == 7)), reads=[r_di, r_xt], writes=[r_yd])
                    yds.append((yd, r_yd))
                xws = []
                if c < 31:
                    for g in range(2):
                        xw, r_xw = xw_ring.next()
                        hs = g * 8
                        pb.op("dve", lambda e, xw=xw, xt=xt, g=g, c=c, hs=hs: e.tensor_tensor(
                            out=xw[:].rearrange("p (k d) -> p k d", d=64), in0=xt[:, g * 512:(g + 1) * 512].rearrange("p (k d) -> p k d", d=64),
                            in1=bc_last(wst[:, c, hs:hs + 8], 64), op=ALU.mult), reads=[r_xt, r_wst], writes=[r_xw])
                        ps, r_ps = sps_ring.next()
                        pb.op("pe", lambda e, ps=ps, bt=bt, xw=xw, g=g: e.matmul(
                            ps[:], lhsT=bt[:, g * 128:(g + 1) * 128], rhs=xw[:], start=True, stop=True),
                            reads=[r_bt, r_xw], writes=[r_ps])
                        xws.append((ps, r_ps))
                yzs = []
                for g in range(2):
                    yd, r_yd = yds[g]
                    yz, r_yz = yz_ring.next()
                    pb.op("dve", lambda e, yz=yz, yd=yd, zt=zt, g=g: e.tensor_tensor(
                        out=yz[:], in0=yd[:], in1=zt[:, g * 512:(g + 1) * 512], op=ALU.mult), reads=[r_yd, r_zt], writes=[r_yz])
                    sq, r_sq = sq_ring.next()
                    ss, r_ss = ss_ring.next()
                    pb.op("act", lambda e, sq=sq, yz=yz, ss=ss: e.activation(
                        out=sq[:], in_=yz[:], func=AF.Square, accum_out=ss[:, 0:1]), reads=[r_yz], writes=[r_sq, r_ss])
                    yzs.append((yz, r_yz, ss, r_ss))
                if c < 31:
                    for g in range(2):
                        hs = g * 8
                        pb.op("dve", lambda e, g=g, c=c, hs=hs: e.tensor_tensor(
                            out=hf[:, g, :].rearrange("p (k d) -> p k d", d=64), in0=hf[:, g, :].rearrange("p (k d) -> p k d", d=64),
                            in1=bc_last(cdb[:, c, hs:hs + 8], 64), op=ALU.mult), reads=[r_hf, r_cdb], writes=[r_hf])
                for g in range(2):
                    yz, r_yz, ss, r_ss = yzs[g]
                    pb.op("dve", lambda e, ss=ss: e.tensor_scalar(
                        out=ss[:, 1:2], in0=ss[:, 0:1], scalar1=1.0 / 512, scalar2=EPS, op0=ALU.mult, op1=ALU.add),
                        reads=[r_ss], writes=[r_ss])
                    pb.op("act", lambda e, ss=ss: e.activation(out=ss[:, 1:2], in_=ss[:, 1:2], func=AF.Sqrt),
                          reads=[r_ss], writes=[r_ss])
                if c < 31:
                    for g in range(2):
                        ps, r_ps = xws[g]
                        pb.op("dve", lambda e, g=g, ps=ps: e.tensor_tensor(out=hf[:, g, :], in0=hf[:, g, :], in1=ps[:], op=ALU.add),
                              reads=[r_hf, r_ps], writes=[r_hf])
                        pb.op("act", lambda e, g=g: e.activation(out=hfb[:, g, :], in_=hf[:, g, :], func=AF.Copy),
                              reads=[r_hf], writes=[r_hfb])
                for g in range(2):
                    yz, r_yz, ss, r_ss = yzs[g]
                    pb.op("dve", lambda e, ss=ss: e.reciprocal(out=ss[:, 1:2], in_=ss[:, 1:2]), reads=[r_ss], writes=[r_ss])
                    pb.op("dve", lambda e, o=o, yz=yz, ss=ss, g=g: e.scalar_tensor_tensor(
                        out=o[:, g * 512:(g + 1) * 512], in0=yz[:], scalar=ss[:, 1:2], in1=nwb[:, g * 512:(g + 1) * 512],
                        op0=ALU.mult, op1=ALU.mult), reads=[r_yz, r_ss, r_nwb], writes=[r_o])
                pb.dma(mixed[cs_, 512:1536], o[:], reads=[r_o], writes=[R["mixed"]])

            ld_cur = sw_load(0)
            mt_cur = sw_dphase(0)
            for c in range(32):
                if c + 1 < 32:
                    ld_nxt = sw_load(c + 1)
                    mt_nxt = sw_dphase(c + 1)
                sw_ystate(c, ld_cur, *mt_cur)
                if c < 31:
                    ld_cur, mt_cur = ld_nxt, mt_nxt


def phase_fnet(nc, pb, L, R, env):
    g_ = lambda k: env[k]
    u_tok, gf, mixed, zbuf = g_("u_tok"), g_("gf"), g_("mixed"), g_("zbuf")
    w_fourier, fcs_d, ccs_d, w3_d = g_("w_fourier"), g_("fcs_d"), g_("ccs_d"), g_("w3_d")
    KAP = 1.0 / float(np.sqrt(4096.0 * 128.0))
    with ExitStack() as st:
        fcs, r_fcs = sb(st, nc, "fn_fcs", [128, 32, 256], BF16)
        w3, r_w3 = sb(st, nc, "fn_w3", [128, 64], BF16)
        pb.dma(fcs[:, 0:16, :], fcs_d[:, 0:16, :], writes=[r_fcs])
        pb.dma(fcs[:, 16:32, :], fcs_d[:, 16:32, :], writes=[r_fcs])
        pb.dma(w3[:], w3_d[:, :], writes=[r_w3])
        with ExitStack() as s1:
            ccs, r_ccs = sb(s1, nc, "fn_ccs", [128, 2, 128], F32)
            wf, r_wf = sb(s1, nc, "fn_wf", [128, 4, 128], F32)
            w12, r_w12 = sb(s1, nc, "fn_w12", [128, 4, 2, 256], BF16)
            ut, r_ut = sb(s1, nc, "fn_u", [128, 32, 512], BF16)
            pb.dma(ccs[:], ccs_d[:, :, :], writes=[r_ccs])
            pb.dma(wf[:], w_fourier[L].rearrange("g c d -> c g d"), writes=[r_wf])
            pb.dma(ut[:, 0:16, :], u_tok.rearrange("(a b) c -> a b c", b=32)[:, 0:16, :], reads=[R["u_tok"]], writes=[r_ut])
            pb.dma(ut[:, 16:32, :], u_tok.rearrange("(a b) c -> a b c", b=32)[:, 16:32, :], reads=[R["u_tok"]], writes=[r_ut])
            ps_ring = Ring(s1, nc, "fn_ps", [128, 512], F32, 4, psum=True)
            yt_ring = Ring(s1, nc, "fn_yt", [128, 32, 256], BF16, 2)
            zs_ring = Ring(s1, nc, "fn_zs", [128, 8, 256], BF16, 2)
            for g in range(4):
                ps, r_ps = ps_ring.next()
                for ri in range(2):
                    pb.op("pe", lambda e, ps=ps, ri=ri, g=g: e.matmul(
                        ps[:, ri * 128:(ri + 1) * 128], lhsT=ccs[:, ri, :], rhs=wf[:, g, :], start=True, stop=True),
                        reads=[r_ccs, r_wf], writes=[r_ps])
                pb.op("act", lambda e, ps=ps, g=g: e.activation(out=w12[:, g, 0, :], in_=ps[:, 0:256], func=AF.Copy, scale=KAP),
                      reads=[r_ps], writes=[r_w12])
                pb.op("act", lambda e, ps=ps, g=g: e.activation(out=w12[:, g, 1, 0:128], in_=ps[:, 128:256], func=AF.Copy, scale=-KAP),
                      reads=[r_ps], writes=[r_w12])
                pb.op("act", lambda e, ps=ps, g=g: e.activation(out=w12[:, g, 1, 128:256], in_=ps[:, 0:128], func=AF.Copy, scale=KAP),
                      reads=[r_ps], writes=[r_w12])
            ne = 0
            prev_yt_readers = {}
            for g in range(4):
                yt, _r_yt_unused = yt_ring.next()
                r_ytl = [Res("yt%d" % b2_) for b2_ in range(16)]
                if g >= 2:
                    for b2_ in range(16):
                        r_ytl[b2_].r = dict(prev_yt_readers[g - 2])
                for b2 in range(16):
                    r_yt = r_ytl[b2]
                    ps, r_ps = ps_ring.next()
                    for i in range(2):
                        b = b2 * 2 + i
                        pb.op("pe", lambda e, ps=ps, i=i, b=b, g=g: e.matmul(
                            ps[:, i * 256:(i + 1) * 256], lhsT=ut[:, b, g * 128:(g + 1) * 128], rhs=fcs[:, b, :],
                            start=True, stop=True), reads=[r_ut, r_fcs], writes=[r_ps])
                    ne += 1
                    dst = yt[:, b2 * 2:b2 * 2 + 2, :].rearrange("p a b -> p (a b)")
                    if ne % 2 == 0:
                        pb.op("act", lambda e, dst=dst, ps=ps: e.activation(out=dst, in_=ps[:], func=AF.Copy), reads=[r_ps], writes=[r_yt])
                    else:
                        pb.op("dve", lambda e, dst=dst, ps=ps: e.tensor_copy(out=dst, in_=ps[:]), reads=[r_ps], writes=[r_yt])
                for b8 in range(4):
                    zst, r_zst = zs_ring.next()
                    for b2 in range(4):
                        ps, r_ps = ps_ring.next()
                        for i in range(2):
                            b = b8 * 8 + b2 * 2 + i
                            r_yt = r_ytl[b // 2]
                            pb.op("pe", lambda e, ps=ps, i=i, b=b, g=g, yt=yt: e.matmul(
                                ps[:, i * 256:(i + 1) * 256], lhsT=yt[:, b, 0:128], rhs=w12[:, g, 0, :], start=True, stop=False),
                                reads=[r_yt, r_w12], writes=[r_ps])
                            pb.op("pe", lambda e, ps=ps, i=i, b=b, g=g, yt=yt: e.matmul(
                                ps[:, i * 256:(i + 1) * 256], lhsT=yt[:, b, 128:256], rhs=w12[:, g, 1, :], start=False, stop=True),
                                reads=[r_yt, r_w12], writes=[r_ps])
                        ne += 1
                        dst = zst[:, b2 * 2:b2 * 2 + 2, :].rearrange("p a b -> p (a b)")
                        if ne % 2 == 0:
                            pb.op("act", lambda e, dst=dst, ps=ps: e.activation(out=dst, in_=ps[:], func=AF.Copy), reads=[r_ps], writes=[r_zst])
                        else:
                            pb.op("dve", lambda e, dst=dst, ps=ps: e.tensor_copy(out=dst, in_=ps[:]), reads=[r_ps], writes=[r_zst])
                    for ri in range(2):
                        pb.dma(zbuf[:, b8 * 8:(b8 + 1) * 8, ri, g, :], zst[:, :, ri * 128:(ri + 1) * 128],
                               reads=[r_zst], writes=[R["zbuf"]])
                mrg = {}
                for r_ in r_ytl:
                    for k_, v_ in r_.r.items():
                        if mrg.get(k_, 0) < v_:
                            mrg[k_] = v_
                prev_yt_readers[g] = mrg
        pb.barrier()
        with ExitStack() as s2:
            zr_ring = Ring(s2, nc, "fn_zr", [128, 8, 512], BF16, 2)
            gt_ring = Ring(s2, nc, "fn_gt", [64, 8, 512], BF16, 2)
            ot_ring = Ring(s2, nc, "fn_ot", [64, 8, 512], BF16, 2)
            ps_ring = Ring(s2, nc, "fn_ps2", [128, 512], F32, 4, psum=True)
            gf3 = gf.rearrange("(bp r) c -> bp r c", r=128)
            mx3 = mixed.rearrange("(bp r) c -> bp r c", r=128)
            for ab in range(8):
                zr, r_zr = zr_ring.next()
                gt, r_gt = gt_ring.next()
                ot, r_ot = ot_ring.next()
                for a2 in range(2):
                    a0 = a2 * 64 + ab * 8
                    for ri in range(2):
                        p0 = a2 * 64 + ri * 32
                        pb.dma(zr[p0:p0 + 32, :, :],
                               zbuf[a0:a0 + 8, :, ri, :, :].rearrange("a b g d -> b a (g d)"),
                               reads=[R["zbuf"]], writes=[r_zr])
                    pb.dma(gt[a2 * 32:(a2 + 1) * 32, :, :], gf3[:, a0:a0 + 8, :], reads=[R["gf"]], writes=[r_gt])
                for ai in range(8):
                    ps, r_ps = ps_ring.next()
                    pb.op("pe", lambda e, ps=ps, zr=zr, ai=ai: e.matmul(
                        ps[0:64, :], lhsT=w3[:], rhs=zr[:, ai, :], start=True, stop=True), reads=[r_w3, r_zr], writes=[r_ps])
                    pb.op("dve", lambda e, ot=ot, ps=ps, gt=gt, ai=ai: e.tensor_tensor(
                        out=ot[:, ai, :], in0=ps[0:64, :], in1=gt[:, ai, :], op=ALU.mult), reads=[r_ps, r_gt], writes=[r_ot])
                for a2 in range(2):
                    a0 = a2 * 64 + ab * 8
                    pb.dma(mx3[:, a0:a0 + 8, 1536:2048], ot[a2 * 32:(a2 + 1) * 32, :, :], reads=[r_ot], writes=[R["mixed"]])


def _make_rpbT(na_rpb):
    nl = na_rpb.shape[0]
    kc = np.arange(64)[:, None]
    qc = np.arange(64)[None, :]
    c0 = np.clip(qc - 8, 0, 48)
    valid = (kc >= c0) & (kc < c0 + 16)
    idx = np.clip(kc - qc + 15, 0, 30)
    g = na_rpb[:, :, :, idx]
    g = np.where(valid[None, None, None], g, np.float32(-30000.0)).astype(np.float32)
    return np.ascontiguousarray(g.transpose(0, 1, 3, 2, 4))


def _constants():
    d = {"ident": np.eye(128, dtype=np.float32)}
    m = np.arange(128)[:, None]
    s_ = np.arange(128)[None, :]
    d["tri_f"] = (m <= s_).astype(np.float32)
    d["tri_b"] = (m >= s_).astype(np.float32)
    sel = np.zeros((128, 32, 128), np.float32)
    for k in range(96):
        sel[k, k % 32, :] = 1.0
    l_ = np.arange(128)
    mrows = np.zeros((32, 128), np.float32)
    for k in range(15):
        tau = 8 * (k + 1)
        sel[96 + k, 0:16, :] = np.where(l_ < tau, -30000.0, 0.0)[None, :]
        sel[112 + k, 16:32, :] = np.where(l_ >= tau, -30000.0, 0.0)[None, :]
        mrows[k, :] = (l_ >= tau).astype(np.float32)
        mrows[16 + k, :] = (l_ < tau).astype(np.float32)
    d["sel"] = sel.reshape(128, 32 * 128)
    d["mrows"] = mrows.astype(ml_dtypes.bfloat16)
    a = np.arange(128)[:, None, None].astype(np.float64)
    b = np.arange(32)[None, :, None].astype(np.float64)
    ap = np.arange(128)[None, None, :].astype(np.float64)
    ang = 2.0 * np.pi * ((ap * (32.0 * a + b)) % 4096.0) / 4096.0
    fcs = np.stack([np.cos(ang), np.sin(ang)], axis=2)
    d["fcs"] = fcs.reshape(128, 32, 256).astype(np.float32).astype(ml_dtypes.bfloat16)
    c = np.arange(128)[:, None].astype(np.float64)
    cp = np.arange(128)[None, :].astype(np.float64)
    angc = 2.0 * np.pi * ((c * cp) % 128.0) / 128.0
    d["ccs"] = np.ascontiguousarray(np.stack([np.cos(angc), np.sin(angc)], axis=1).astype(np.float32))
    w3 = np.zeros((2, 2, 32, 2, 32), np.float64)
    bb = np.arange(32)[:, None] * np.arange(32)[None, :]
    ang3 = 2.0 * np.pi * (bb % 32) / 32.0
    for a2 in range(2):
        w3[a2, 0, :, a2, :] = np.cos(ang3)
        w3[a2, 1, :, a2, :] = -np.sin(ang3)
    d["w3"] = w3.reshape(128, 64).astype(np.float32).astype(ml_dtypes.bfloat16)
    return d


def host_inputs(x, norm_w, w_in, na_rpb, conv_w, conv_b, dt_bias, a_log, d_skip, ssd_norm_w,
                w_fourier, w_out, final_norm_w):
    f = lambda a: np.ascontiguousarray(np.asarray(a, dtype=np.float32))
    nl = norm_w.shape[0]
    sh = {
        "norm_w": f(np.asarray(norm_w).reshape(nl, 8, 128).transpose(0, 2, 1)),
        "w_in": f(w_in), "w_out": f(w_out),
        "final_norm_w": f(np.asarray(final_norm_w)[None]),
        "rpbT": _make_rpbT(f(na_rpb)),
        "conv_wT": f(np.asarray(conv_w).reshape(nl, 5, 12, 128).transpose(0, 2, 3, 1)),
        "conv_bT": f(np.asarray(conv_b).reshape(nl, 12, 128).transpose(0, 2, 1)),
        "dt_bias": f(np.asarray(dt_bias).reshape(nl, 32)),
        "a_log": f(np.asarray(a_log).reshape(nl, 32)),
        "d_skip": f(d_skip), "ssd_norm_w": f(ssd_norm_w), "w_fourier": f(w_fourier),
    }
    sh.update(_constants())
    return sh


_NC_CACHE = {}


def kernel(x, norm_w, w_in, na_rpb, conv_w, conv_b, dt_bias, a_log, d_skip, ssd_norm_w,
           w_fourier, w_out, final_norm_w):
    x = np.asarray(x, dtype=np.float32)
    shared = host_inputs(x, norm_w, w_in, na_rpb, conv_w, conv_b, dt_bias, a_log, d_skip, ssd_norm_w,
                         w_fourier, w_out, final_norm_w)
    if "nc" not in _NC_CACHE:
        _NC_CACHE["nc"] = build_program()
    nc = _NC_CACHE["nc"]
    n = x.shape[0]
    in_maps = []
    for b in range(n):
        m = dict(shared)
        m["x"] = np.ascontiguousarray(x[b])
        in_maps.append(m)
    res = run_bass_kernel_spmd(nc, in_maps, core_ids=list(range(n)))
    return np.stack([np.asarray(r["out"], dtype=np.float32) for r in res.results], axis=0)
```

```python
import numpy as np
import ml_dtypes
import concourse.bass as bass
import concourse.mybir as mybir
from concourse.bass_utils import run_bass_kernel_spmd
from contextlib import ExitStack

F32 = mybir.dt.float32
BF16 = mybir.dt.bfloat16
AF = mybir.ActivationFunctionType
ALU = mybir.AluOpType

S = 4096
D = 1024
NL = 4
INC = 5664
NT = 32
EPS = 1e-6
NDS = 80
BIGF = 1.0e30


class Res:
    __slots__ = ("name", "w", "r", "excl", "multi", "wl")

    def __init__(self, name="", excl=False, multi=False):
        self.name = name
        self.w = None
        self.r = {}
        self.excl = excl
        self.multi = multi
        self.wl = []


class PB:
    ENGS = ("pe", "act", "dve", "pool", "sp")

    def __init__(self, nc, st):
        self.nc = nc
        self.ops = {e: [] for e in self.ENGS}
        self.esem = {e: st.enter_context(nc.semaphore("es_" + e)) for e in ("pe", "act", "dve", "pool")}
        self.ecnt = {e: 0 for e in self.esem}
        self.known = {e: {} for e in self.ENGS}
        self.dsems = [st.enter_context(nc.semaphore("ds%d" % i)) for i in range(NDS)]
        self.dcnt = [0] * NDS
        self.dnext = 0
        self.dnext_p = 0
        self.multis = set()

    def _sem(self, key):
        return self.esem[key] if isinstance(key, str) else self.dsems[key[1]]

    def _need(self, e, key, val, waits):
        if key == e and e == "pe":
            return
        if self.known[e].get(key, 0) >= val:
            return
        self.known[e][key] = val
        waits.append((key, val))

    def _deps(self, e, reads, writes):
        waits = []
        for r in reads:
            if r.w is not None:
                self._need(e, r.w[0], r.w[1], waits)
            for k, v in r.wl:
                self._need(e, k, v, waits)
            if r.excl:
                for k, v in r.r.items():
                    if k != e:
                        self._need(e, k, v, waits)
        for w in writes:
            if w.w is not None and not w.multi:
                self._need(e, w.w[0], w.w[1], waits)
            for k, v in w.r.items():
                self._need(e, k, v, waits)
        return waits

    def _mark(self, key, val, reads, writes):
        for r in reads:
            if r.r.get(key, 0) < val:
                r.r[key] = val
        for w in writes:
            if w.multi:
                w.wl.append((key, val))
                self.multis.add(w)
            w.w = (key, val)
            w.r = {}

    def op(self, e, fn, reads=(), writes=()):
        waits = self._deps(e, reads, writes)
        self.ecnt[e] += 1
        self.ops[e].append((waits, fn, (e, 1)))
        self._mark(e, self.ecnt[e], reads, writes)

    def dma(self, out, in_, reads=(), writes=(), q=None):
        if q is None:
            q = "pool" if "DRAM" in str(out.space) else "sp"
        waits = self._deps(q, reads, writes)
        half = NDS // 2
        if q == "pool":
            i = half + self.dnext_p
            self.dnext_p = (self.dnext_p + 1) % half
        else:
            i = self.dnext
            self.dnext = (self.dnext + 1) % half
        key = ("d", i)
        if self.dcnt[i] > 0:
            self._need(q, key, self.dcnt[i], waits)
        self.dcnt[i] += 16
        self.ops[q].append((waits, (lambda eng, o=out, s=in_: eng.dma_start(out=o, in_=s)), (key, 16)))
        self._mark(key, self.dcnt[i], reads, writes)

    def barrier(self):
        for e in self.ENGS:
            waits = []
            for k in self.esem:
                if k != e and self.ecnt[k] > 0:
                    self._need(e, k, self.ecnt[k], waits)
            for i in range(NDS):
                if self.dcnt[i] > 0:
                    self._need(e, ("d", i), self.dcnt[i], waits)
            if waits:
                self.ops[e].append((waits, None, None))
        for w in self.multis:
            w.wl = []
        self.multis = set()

    def emit(self):
        nc = self.nc
        with nc.Block() as block:
            def run(e):
                def body(eng):
                    for waits, fn, inc in self.ops[e]:
                        for key, val in waits:
                            eng.wait_ge(self._sem(key), val)
                        if fn is not None:
                            fn(eng).then_inc(self._sem(inc[0]), inc[1])
                return body
            block.tensor(run("pe"))
            block.scalar(run("act"))
            block.vector(run("dve"))
            block.gpsimd(run("pool"))
            block.sync(run("sp"))


class Ring:
    def __init__(self, st, nc, name, shape, dtype, n, psum=False):
        self.slots = []
        name = _uniq(name)
        for i in range(n):
            if psum:
                assert int(np.prod(shape[1:])) * (4 if dtype == F32 else 2) == 2048 and shape[0] == 128, shape
                t = st.enter_context(nc.psum_tensor("%s%d" % (name, i), shape, dtype))
            else:
                t = st.enter_context(nc.sbuf_tensor("%s%d" % (name, i), shape, dtype))
            self.slots.append((t, Res("%s%d" % (name, i), excl=psum)))
        self.i = 0

    def next(self):
        s = self.slots[self.i]
        self.i = (self.i + 1) % len(self.slots)
        return s


_UID = [0]


def _uniq(name):
    _UID[0] += 1
    return "%s_u%d" % (name, _UID[0])


def sb(st, nc, name, shape, dtype):
    return st.enter_context(nc.sbuf_tensor(_uniq(name), shape, dtype)), Res(name)


CQ, CK, CV, CG, CZ, CX, CDT, CF, CGF = 0, 512, 1024, 1536, 2048, 3072, 4608, 4640, 5152


def build_program(n_layers=NL, debug=None):
    nc = bass.Bass("TRN2", target_bir_lowering=False)
    dbg = debug or {}

    def din(name, shape, dt=F32):
        return nc.dram_tensor(name, list(shape), dt, kind="ExternalInput").ap()

    def dscr(name, shape, dt):
        kind = "ExternalOutput" if name in dbg else "Internal"
        return nc.dram_tensor(name, list(shape), dt, kind=kind).ap()

    x_in = din("x", [S, D])
    norm_w = din("norm_w", [NL, 128, 8])
    w_in = din("w_in", [NL, D, INC])
    w_out = din("w_out", [NL, 2048, D])
    final_norm_w = din("final_norm_w", [1, D])
    ident_d = din("ident", [128, 128])
    rpbT = din("rpbT", [NL, 8, 64, 15, 64])
    conv_wT = din("conv_wT", [NL, 12, 128, 5])
    conv_bT = din("conv_bT", [NL, 128, 12])
    dt_bias = din("dt_bias", [NL, 32])
    a_log = din("a_log", [NL, 32])
    d_skip = din("d_skip", [NL, 16])
    ssd_norm_w = din("ssd_norm_w", [NL, 1024])
    tri_f_d = din("tri_f", [128, 128])
    tri_b_d = din("tri_b", [128, 128])
    sel_d = din("sel", [128, 32 * 128])
    mrows_d = din("mrows", [32, 128], BF16)
    w_fourier = din("w_fourier", [NL, 4, 128, 128])
    fcs_d = din("fcs", [128, 32, 256], BF16)
    ccs_d = din("ccs", [128, 2, 128])
    w3_d = din("w3", [128, 64], BF16)
    out_d = nc.dram_tensor("out", [S, D], F32, kind="ExternalOutput").ap()

    xres = dscr("xres", [S, D], F32)
    qT = dscr("qT", [4, 128, S], BF16)
    kT = dscr("kT", [4, 128, S], BF16)
    v_tok = dscr("v_tok", [S, 512], BF16)
    gna = dscr("gna", [S, 512], BF16)
    zs = dscr("zs", [S, 1024], BF16)
    xbcT = dscr("xbcT", [12, 128, S], BF16)
    dtraw = dscr("dtraw", [S, 32], F32)
    u_tok = dscr("u_tok", [S, 512], BF16)
    gf = dscr("gf", [S, 512], BF16)
    mixed = dscr("mixed", [S, 2048], BF16)
    x_tok = dscr("x_tok", [S, 1024], BF16)
    b_tok = dscr("b_tok", [S, 256], BF16)
    hb_d = dscr("hb_d", [32, 128, 1024], BF16)
    zbuf = dscr("zbuf", [128, 32, 2, 4, 128], BF16)

    R = {n: Res(n, multi=True) for n in ["xres", "qT", "kT", "v_tok", "gna", "zs", "xbcT", "dtraw", "u_tok", "gf",
                             "mixed", "out", "win", "wout", "x_tok", "b_tok", "hb_d", "zbuf"]}

    with ExitStack() as gst:
        pb = PB(nc, gst)
        ident_f, r_identf = sb(gst, nc, "ident_f", [128, 128], F32)
        ident_b, r_identb = sb(gst, nc, "ident_b", [128, 128], BF16)
        pb.dma(ident_f[:], ident_d[:, :], writes=[r_identf])
        pb.op("dve", lambda e: e.tensor_copy(out=ident_b[:], in_=ident_f[:]), reads=[r_identf], writes=[r_identb])

        nw_all, r_nw = sb(gst, nc, "nw_all", [128, NL, 8], F32)
        for L_ in range(NL):
            pb.dma(nw_all[:, L_, :], norm_w[L_], writes=[r_nw])
        outer_env = dict(locals())

        def do_layer(L):
            xsrc = x_in if L == 0 else xres
            last = (L == n_layers - 1)
            with ExitStack() as st:
                wb, r_wb = sb(st, nc, "p1_w", [128, 8, INC], BF16)
                nw = nw_all[:, L, :]
                wst = Ring(st, nc, "p1_wst", [128, 1416], F32, 3)
                n = 0
                for kc in range(8):
                    for qd in range(4):
                        t, r = wst.next()
                        pb.dma(t[:], w_in[L, kc * 128:(kc + 1) * 128, qd * 1416:(qd + 1) * 1416], writes=[r])
                        if n % 2 == 0:
                            pb.op("dve", lambda e, t=t, kc=kc, qd=qd: e.tensor_scalar(
                                out=wb[:, kc, qd * 1416:(qd + 1) * 1416], in0=t[:], scalar1=nw[:, kc:kc + 1],
                                scalar2=None, op0=ALU.mult), reads=[r, r_nw], writes=[r_wb])
                        else:
                            pb.op("act", lambda e, t=t, kc=kc, qd=qd: e.activation(
                                out=wb[:, kc, qd * 1416:(qd + 1) * 1416], in_=t[:], func=AF.Copy, scale=nw[:, kc:kc + 1]),
                                reads=[r, r_nw], writes=[r_wb])
                        n += 1
                xt_ring = Ring(st, nc, "p1_x", [128, D], F32, 8)
                sq_ring = Ring(st, nc, "p1_sq", [128, D], BF16, 2)
                ss_ring = Ring(st, nc, "p1_ss", [128, 2], F32, 8)
                h_ring = Ring(st, nc, "p1_h", [128, D], BF16, 8)
                hT_ring = Ring(st, nc, "p1_hT", [128, 8, 512], BF16, 2)
                tp_ring = Ring(st, nc, "p1_tp", [128, 8, 128], BF16, 2, psum=True)
                mm_ring = Ring(st, nc, "p1_mm", [128, 512], F32, 5, psum=True)
                fo_ring = Ring(st, nc, "p1_fo", [128, 512], BF16, 4)
                to_ring = Ring(st, nc, "p1_to", [128, 3072], BF16, 3)
                to_res = {i: [Res("to%d_%d" % (i, b_)) for b_ in range(6)] for i in range(3)}
                dt_ring = Ring(st, nc, "p1_dt", [128, 32], F32, 2)
                ev = [0]

                def evac(out_ap, ps, r_ps, r_out, silu=False):
                    if silu:
                        pb.op("act", lambda e: e.activation(out=out_ap, in_=ps, func=AF.Silu), reads=[r_ps], writes=[r_out])
                    else:
                        ev[0] += 1
                        if ev[0] % 2 == 0:
                            pb.op("act", lambda e: e.activation(out=out_ap, in_=ps, func=AF.Copy), reads=[r_ps], writes=[r_out])
                        else:
                            pb.op("dve", lambda e: e.tensor_copy(out=out_ap, in_=ps), reads=[r_ps], writes=[r_out])

                def p1_loads(tb):
                    tiles = []
                    for j in range(4):
                        t0 = tb * 512 + j * 128
                        xt, r_xt = xt_ring.next()
                        pb.dma(xt[:], xsrc[t0:t0 + 128, :], reads=[R["xres"]], writes=[r_xt])
                        tiles.append((xt, r_xt))
                    return tiles

                def p1_stats(tiles):
                    hs = []
                    for j in range(4):
                        xt, r_xt = tiles[j]
                        sq, r_sq = sq_ring.next()
                        ss, r_ss = ss_ring.next()
                        pb.op("act", lambda e, sq=sq, xt=xt, ss=ss: e.activation(
                            out=sq[:], in_=xt[:], func=AF.Square, accum_out=ss[:, 0:1]), reads=[r_xt], writes=[r_sq, r_ss])
                        pb.op("dve", lambda e, ss=ss: e.tensor_scalar(
                            out=ss[:, 1:2], in0=ss[:, 0:1], scalar1=1.0 / D, scalar2=EPS, op0=ALU.mult, op1=ALU.add),
                            reads=[r_ss], writes=[r_ss])
                        pb.op("act", lambda e, ss=ss: e.activation(out=ss[:, 1:2], in_=ss[:, 1:2], func=AF.Sqrt),
                              reads=[r_ss], writes=[r_ss])
                        pb.op("dve", lambda e, ss=ss: e.reciprocal(out=ss[:, 1:2], in_=ss[:, 1:2]),
                              reads=[r_ss], writes=[r_ss])
                        h, r_h = h_ring.next()
                        pb.op("dve", lambda e, h=h, xt=xt, ss=ss: e.tensor_scalar(
                            out=h[:], in0=xt[:], scalar1=ss[:, 1:2], scalar2=None, op0=ALU.mult),
                            reads=[r_xt, r_ss], writes=[r_h])
                        hs.append((h, r_h))
                    return hs

                def p1_tr(hs):
                    hT, r_hT = hT_ring.next()
                    for j in range(4):
                        h, r_h = hs[j]
                        tp, r_tp = tp_ring.next()
                        for kc in range(8):
                            pb.op("pe", lambda e, tp=tp, h=h, kc=kc: e.transpose(
                                out=tp[:, kc, :], in_=h[:, kc * 128:(kc + 1) * 128], identity=ident_b[:]),
                                reads=[r_h, r_identb], writes=[r_tp])
                        pb.op("act", lambda e, hT=hT, tp=tp, j=j: e.activation(
                            out=hT[:, :, j * 128:(j + 1) * 128], in_=tp[:], func=AF.Copy), reads=[r_tp], writes=[r_hT])
                    return hT, r_hT

                nxt = p1_tr(p1_stats(p1_loads(0)))
                nxt_tiles = p1_loads(1)
                for tb in range(8):
                    hT, r_hT = nxt
                    if tb + 1 < 8:
                        nxt_hs = p1_stats(nxt_tiles)
                        if tb + 2 < 8:
                            nxt_tiles = p1_loads(tb + 2)
                    for ci in range(20):
                        if ci < 4:
                            c0, dst, rd = CQ + ci * 128, qT[ci], R["qT"]
                        elif ci < 8:
                            c0, dst, rd = CK + (ci - 4) * 128, kT[ci - 4], R["kT"]
                        else:
                            c0, dst, rd = CX + (ci - 8) * 128, xbcT[ci - 8], R["xbcT"]
                        ps, r_ps = mm_ring.next()
                        for kc in range(8):
                            pb.op("pe", lambda e, ps=ps, kc=kc, c0=c0, hT=hT: e.matmul(
                                ps[:], lhsT=wb[:, kc, c0:c0 + 128], rhs=hT[:, kc, :], start=(kc == 0), stop=(kc == 7)),
                                reads=[r_wb, r_hT], writes=[r_ps])
                        fo, r_fo = fo_ring.next()
                        evac(fo[:], ps[:], r_ps, r_fo)
                        pb.dma(dst[:, tb * 512:(tb + 1) * 512], fo[:], reads=[r_fo], writes=[rd])
                    if tb + 1 < 8:
                        nxt = p1_tr(nxt_hs)
                    for j in range(4):
                        t0 = tb * 512 + j * 128
                        to, _r_to_unused = to_ring.next()
                        to_i = to_ring.i
                        blocks = [(CV, 0, False), (CG, 512, True), (CZ, 1024, True), (CZ + 512, 1536, True),
                                  (CF, 2048, False), (CGF, 2560, True)]
                        for bi, (c0, o0, silu) in enumerate(blocks):
                            r_to = to_res[to_i][bi]
                            ps, r_ps = mm_ring.next()
                            for kc in range(8):
                                pb.op("pe", lambda e, ps=ps, kc=kc, c0=c0, hT=hT, j=j: e.matmul(
                                    ps[:], lhsT=hT[:, kc, j * 128:(j + 1) * 128], rhs=wb[:, kc, c0:c0 + 512],
                                    start=(kc == 0), stop=(kc == 7)), reads=[r_wb, r_hT], writes=[r_ps])
                            evac(to[:, o0:o0 + 512], ps[:], r_ps, r_to, silu=silu)
                        ps, r_ps = mm_ring.next()
                        for kc in range(8):
                            pb.op("pe", lambda e, ps=ps, kc=kc, hT=hT, j=j: e.matmul(
                                ps[:, 0:32], lhsT=hT[:, kc, j * 128:(j + 1) * 128], rhs=wb[:, kc, CDT:CDT + 32],
                                start=(kc == 0), stop=(kc == 7)), reads=[r_wb, r_hT], writes=[r_ps])
                        dtt, r_dtt = dt_ring.next()
                        pb.op("dve", lambda e, dtt=dtt, ps=ps: e.tensor_copy(out=dtt[:], in_=ps[:, 0:32]),
                              reads=[r_ps], writes=[r_dtt])
                        pb.dma(dtraw[t0:t0 + 128, :], dtt[:], reads=[r_dtt], writes=[R["dtraw"]])
                        rr = to_res[to_i]
                        pb.dma(v_tok[t0:t0 + 128, :], to[:, 0:512], reads=[rr[0]], writes=[R["v_tok"]])
                        pb.dma(gna[t0:t0 + 128, :], to[:, 512:1024], reads=[rr[1]], writes=[R["gna"]])
                        pb.dma(zs[t0:t0 + 128, :], to[:, 1024:2048], reads=[rr[2], rr[3]], writes=[R["zs"]])
                        pb.dma(u_tok[t0:t0 + 128, :], to[:, 2048:2560], reads=[rr[4]], writes=[R["u_tok"]])
                        pb.dma(gf[t0:t0 + 128, :], to[:, 2560:3072], reads=[rr[5]], writes=[R["gf"]])
            pb.barrier()
            if dbg.get("stop") == "p1":
                return True

            env = dict(outer_env)
            env.update(locals())
            if "skip_na" not in dbg:
                phase_na(nc, pb, L, R, env)
                pb.barrier()
            if "skip_ssd" not in dbg:
                phase_ssd(nc, pb, L, R, env)
                pb.barrier()
            if "skip_fnet" not in dbg:
                phase_fnet(nc, pb, L, R, env)
                pb.barrier()
            if dbg.get("stop") == "mix":
                return True

            with ExitStack() as st:
                wo, r_wo = sb(st, nc, "p5_w", [128, 16, D], BF16)
                wst = Ring(st, nc, "p5_wst", [128, D], F32, 3)
                for kc in range(16):
                    t, r = wst.next()
                    pb.dma(t[:], w_out[L, kc * 128:(kc + 1) * 128, :], writes=[r])
                    if kc % 2 == 0:
                        pb.op("dve", lambda e, t=t, kc=kc: e.tensor_copy(out=wo[:, kc, :], in_=t[:]), reads=[r], writes=[r_wo])
                    else:
                        pb.op("act", lambda e, t=t, kc=kc: e.activation(out=wo[:, kc, :], in_=t[:], func=AF.Copy), reads=[r], writes=[r_wo])
                if last:
                    fnw, r_fnw = sb(st, nc, "p5_fnw", [128, D], F32)
                    pb.dma(fnw[:], final_norm_w[0:1, :].broadcast_to([128, D]), writes=[r_fnw])
                m_ring = Ring(st, nc, "p5_m", [128, 2048], BF16, 3)
                x_ring = Ring(st, nc, "p5_x", [128, D], F32, 4)
                mT_ring = Ring(st, nc, "p5_mT", [128, 16, 128], BF16, 3)
                tp_ring = Ring(st, nc, "p5_tp", [128, 8, 128], BF16, 2, psum=True)
                mm_ring = Ring(st, nc, "p5_mm", [128, 512], F32, 4, psum=True)
                xo_ring = Ring(st, nc, "p5_xo", [128, D], F32, 2)
                sq_ring = Ring(st, nc, "p5_sq", [128, D], BF16, 2)
                ss_ring = Ring(st, nc, "p5_ss", [128, 2], F32, 2)
                yo_ring = Ring(st, nc, "p5_yo", [128, D], F32, 2)
                def p5_prep(t):
                    t0 = t * 128
                    m, r_m = m_ring.next()
                    pb.dma(m[:], mixed[t0:t0 + 128, :], reads=[R["mixed"]], writes=[r_m])
                    xt, r_xt = x_ring.next()
                    pb.dma(xt[:], xsrc[t0:t0 + 128, :], reads=[R["xres"]], writes=[r_xt])
                    mT, r_mT = mT_ring.next()
                    for hh in range(2):
                        tp, r_tp = tp_ring.next()
                        for kc in range(8):
                            kk = hh * 8 + kc
                            pb.op("pe", lambda e, tp=tp, m=m, kc=kc, kk=kk: e.transpose(
                                out=tp[:, kc, :], in_=m[:, kk * 128:(kk + 1) * 128], identity=ident_b[:]),
                                reads=[r_m, r_identb], writes=[r_tp])
                        if hh == 0:
                            pb.op("act", lambda e, mT=mT, tp=tp: e.activation(
                                out=mT[:, 0:8, :], in_=tp[:], func=AF.Copy), reads=[r_tp], writes=[r_mT])
                        else:
                            pb.op("dve", lambda e, mT=mT, tp=tp: e.tensor_copy(out=mT[:, 8:16, :], in_=tp[:]),
                                  reads=[r_tp], writes=[r_mT])
                    return xt, r_xt, mT, r_mT

                nxt5 = p5_prep(0)
                for t in range(NT):
                    t0 = t * 128
                    xt, r_xt, mT, r_mT = nxt5
                    xo, r_xo = xo_ring.next()
                    for cb in range(2):
                        if cb == 1 and t + 1 < NT:
                            nxt5 = p5_prep(t + 1)
                        ps, r_ps = mm_ring.next()
                        for kc in range(16):
                            pb.op("pe", lambda e, ps=ps, kc=kc, cb=cb, mT=mT: e.matmul(
                                ps[:], lhsT=mT[:, kc, :], rhs=wo[:, kc, cb * 512:(cb + 1) * 512],
                                start=(kc == 0), stop=(kc == 15)), reads=[r_wo, r_mT], writes=[r_ps])
                        pb.op("dve", lambda e, xo=xo, ps=ps, xt=xt, cb=cb: e.tensor_tensor(
                            out=xo[:, cb * 512:(cb + 1) * 512], in0=ps[:], in1=xt[:, cb * 512:(cb + 1) * 512], op=ALU.add),
                            reads=[r_ps, r_xt], writes=[r_xo])
                    if not last:
                        pb.dma(xres[t0:t0 + 128, :], xo[:], reads=[r_xo], writes=[R["xres"]])
                    else:
                        sq, r_sq = sq_ring.next()
                        ss, r_ss = ss_ring.next()
                        pb.op("act", lambda e, sq=sq, xo=xo, ss=ss: e.activation(
                            out=sq[:], in_=xo[:], func=AF.Square, accum_out=ss[:, 0:1]), reads=[r_xo], writes=[r_sq, r_ss])
                        pb.op("dve", lambda e, ss=ss: e.tensor_scalar(
                            out=ss[:, 1:2], in0=ss[:, 0:1], scalar1=1.0 / D, scalar2=EPS, op0=ALU.mult, op1=ALU.add),
                            reads=[r_ss], writes=[r_ss])
                        pb.op("act", lambda e, ss=ss: e.activation(out=ss[:, 1:2], in_=ss[:, 1:2], func=AF.Sqrt),
                              reads=[r_ss], writes=[r_ss])
                        pb.op("dve", lambda e, ss=ss: e.reciprocal(out=ss[:, 1:2], in_=ss[:, 1:2]),
                              reads=[r_ss], writes=[r_ss])
                        yo, r_yo = yo_ring.next()
                        pb.op("dve", lambda e, yo=yo, xo=xo, ss=ss: e.scalar_tensor_tensor(
                            out=yo[:], in0=xo[:], scalar=ss[:, 1:2], in1=fnw[:], op0=ALU.mult, op1=ALU.mult),
                            reads=[r_xo, r_ss, r_fnw], writes=[r_yo])
                        pb.dma(out_d[t0:t0 + 128, :], yo[:], reads=[r_yo], writes=[R["out"]])
            pb.barrier()
            return False

        for L in range(n_layers):
            if do_layer(L):
                break
        pb.barrier()
        pb.emit()
    return nc


def phase_na(nc, pb, L, R, env):
    qT, kT, v_tok, gna, mixed, rpbT = (env[k] for k in ["qT", "kT", "v_tok", "gna", "mixed", "rpbT"])
    with ExitStack() as st:
        ets, r_ets = sb(st, nc, "na_ets", [128, 8, 14, 64], BF16)
        tst = Ring(st, nc, "na_tst", [128, 14, 64], F32, 2)
        for h in range(8):
            t, r = tst.next()
            pb.dma(t[0:64], rpbT[L, h, :, 0:14, :], writes=[r])
            pb.dma(t[64:128], rpbT[L, h, :, 1:15, :], writes=[r])
            pb.op("act", lambda e, t=t, h=h: e.activation(out=ets[:, h], in_=t[:], func=AF.Exp), reads=[r], writes=[r_ets])
        kt_ring = Ring(st, nc, "na_kt", [128, S], BF16, 2)
        qc_ring = Ring(st, nc, "na_qc", [128, 64, 128], BF16, 2)
        ve_ring = Ring(st, nc, "na_ve", [128, 32, 2, 65], BF16, 2)
        vo_ring = Ring(st, nc, "na_vo", [128, 32, 2, 65], BF16, 2)
        for (t, r) in qc_ring.slots:
            pb.op("dve", lambda e, t=t: e.memset(t[64:128, :, 0:64], 0.0), writes=[r])
            pb.op("dve", lambda e, t=t: e.memset(t[0:64, :, 64:128], 0.0), writes=[r])
        for (t, r) in ve_ring.slots + vo_ring.slots:
            pb.op("dve", lambda e, t=t: e.memset(t[:, :, :, 64:65], 1.0), writes=[r])
        sp_ring = Ring(st, nc, "na_sp", [128, 4, 2, 64], F32, 4, psum=True)
        op_ring = Ring(st, nc, "na_op", [128, 512], F32, 4, psum=True)
        ex_ring = Ring(st, nc, "na_ex", [128, 4, 2, 64], BF16, 8)
        pt_ring = Ring(st, nc, "na_pt", [128, 4, 2, 64], BF16, 9)
        rc_ring = Ring(st, nc, "na_rc", [64, 16], F32, 2)
        os_ring = Ring(st, nc, "na_os", [64, 8, 130], F32, 2)
        tn_ring = Ring(st, nc, "na_tn", [64, 16, 64], F32, 2)
        g_ring = Ring(st, nc, "na_g", [64, 8, 128], BF16, 2)
        nr_ring = Ring(st, nc, "na_nr", [64, 128], BF16, 4)
        o_ring = Ring(st, nc, "na_o", [64, 8, 128], BF16, 2)
        v4 = v_tok.rearrange("(t p) (h d) -> p t h d", p=128, d=64)
        g3 = gna.rearrange("(r p) c -> p r c", p=64)
        m3 = mixed.rearrange("(r p) c -> p r c", p=64)
        for hp in range(4):
            kt, r_kt = kt_ring.next()
            qc, r_qc = qc_ring.next()
            ve, r_ve = ve_ring.next()
            vo, r_vo = vo_ring.next()
            pb.dma(kt[:], kT[hp], reads=[R["kT"]], writes=[r_kt])
            pb.dma(qc[0:64, :, 0:64], qT[hp, 0:64, :].rearrange("p (r q) -> p r q", q=64), reads=[R["qT"]], writes=[r_qc])
            pb.dma(qc[64:128, :, 64:128], qT[hp, 64:128, :].rearrange("p (r q) -> p r q", q=64), reads=[R["qT"]], writes=[r_qc])
            for tq in range(2):
                for hh in range(2):
                    pb.dma(ve[:, tq * 16:(tq + 1) * 16, hh, 0:64], v4[:, tq * 16:(tq + 1) * 16, 2 * hp + hh, :],
                           reads=[R["v_tok"]], writes=[r_ve])
            vsh = v_tok[64:64 + 31 * 128, :].rearrange("(t p) (h d) -> p t h d", p=128, d=64)
            for tq in range(2):
                t1 = min(31, (tq + 1) * 16)
                for hh in range(2):
                    pb.dma(vo[:, tq * 16:t1, hh, 0:64], vsh[:, tq * 16:t1, 2 * hp + hh, :],
                           reads=[R["v_tok"]], writes=[r_vo])
            def na_stage_a(r):
                r0 = min(max(r - 4, 0), 56)
                sp, r_sp = sp_ring.next()
                for c in range(4):
                    k0 = (r0 + 2 * c) * 64
                    pb.op("pe", lambda e, sp=sp, c=c, k0=k0, r=r, kt=kt, qc=qc: e.matmul(
                        sp[:, c, :, :], lhsT=kt[:, k0:k0 + 128], rhs=qc[:, r, :],
                        start=True, stop=True), reads=[r_kt, r_qc], writes=[r_sp])
                ex, r_ex = ex_ring.next()
                pb.op("act", lambda e, ex=ex, sp=sp: e.activation(out=ex[:], in_=sp[:], func=AF.Exp, scale=0.125),
                      reads=[r_sp], writes=[r_ex])
                pt, r_pt = pt_ring.next()
                m0 = r0 - r + 7
                for hh in range(2):
                    pb.op("dve", lambda e, pt=pt, ex=ex, m0=m0, hp=hp, hh=hh: e.tensor_tensor(
                        out=pt[:, :, hh, :], in0=ex[:, :, hh, :], in1=ets[:, 2 * hp + hh, m0:m0 + 7:2, :], op=ALU.mult),
                        reads=[r_ex, r_ets], writes=[r_pt])
                return pt, r_pt

            def na_stage_b(r, pt, r_pt, gt, r_gt, ot, r_ot, ostg):
                r0 = min(max(r - 4, 0), 56)
                ri = r % 8
                o_bank, r_ops = op_ring.next()
                o_ps = o_bank[0:64, 0:130].rearrange("p (h d) -> p h d", d=65)
                for hh in range(2):
                    for c in range(4):
                        row = r0 + 2 * c
                        if row % 2 == 0:
                            vt, r_vt, ti = ve, r_ve, row // 2
                        else:
                            vt, r_vt, ti = vo, r_vo, (row - 1) // 2
                        pb.op("pe", lambda e, o_ps=o_ps, pt=pt, hh=hh, c=c, vt=vt, ti=ti: e.matmul(
                            o_ps[:, hh, :], lhsT=pt[:, c, hh, :], rhs=vt[:, ti, hh, :],
                            start=(c == 0), stop=(c == 3)), reads=[r_pt, r_vt], writes=[r_ops])
                osg, r_osg = ostg
                pb.op("act", lambda e, osg=osg, o_bank=o_bank, ri=ri: e.activation(
                    out=osg[:, ri, :], in_=o_bank[0:64, 0:130], func=AF.Copy), reads=[r_ops], writes=[r_osg])
                if ri == 7:
                    rc, r_rc = rc_ring.next()
                    ov = osg[:].rearrange("p r (h d) -> p (r h) d", d=65)
                    pb.op("dve", lambda e, rc=rc, ov=ov: e.reciprocal(out=rc[:], in_=ov[:, :, 64]),
                          reads=[r_osg], writes=[r_rc])
                    tn, r_tn = tn_ring.next()
                    pb.op("dve", lambda e, tn=tn, ov=ov, rc=rc: e.tensor_tensor(
                        out=tn[:], in0=ov[:, :, 0:64], in1=bc_last(rc[:], 64), op=ALU.mult),
                        reads=[r_osg, r_rc], writes=[r_tn])
                    pb.op("dve", lambda e, ot=ot, tn=tn, gt=gt: e.tensor_tensor(
                        out=ot[:].rearrange("p r c -> p (r c)"), in0=tn[:].rearrange("p a d -> p (a d)"),
                        in1=gt[:].rearrange("p r c -> p (r c)"), op=ALU.mult), reads=[r_tn, r_gt], writes=[r_ot])

            AHEAD = 6
            pend = [na_stage_a(r) for r in range(AHEAD)]
            for rb in range(8):
                gt, r_gt = g_ring.next()
                pb.dma(gt[:], g3[:, rb * 8:(rb + 1) * 8, hp * 128:(hp + 1) * 128], reads=[R["gna"]], writes=[r_gt])
                ot, r_ot = o_ring.next()
                ostg = os_ring.next()
                for ri in range(8):
                    r = rb * 8 + ri
                    if r + AHEAD < 64:
                        pend.append(na_stage_a(r + AHEAD))
                    pt, r_pt = pend.pop(0)
                    na_stage_b(r, pt, r_pt, gt, r_gt, ot, r_ot, ostg)
                pb.dma(m3[:, rb * 8:(rb + 1) * 8, hp * 128:(hp + 1) * 128], ot[:], reads=[r_ot], writes=[R["mixed"]])


def bc_last(ap, n):
    shp = list(ap.shape)
    return ap.unsqueeze(len(shp)).broadcast_to(shp + [n])


def phase_ssd(nc, pb, L, R, env):
    g_ = lambda k: env[k]
    xbcT, dtraw, zs, mixed = g_("xbcT"), g_("dtraw"), g_("zs"), g_("mixed")
    x_tok, b_tok, hb_d = g_("x_tok"), g_("b_tok"), g_("hb_d")
    conv_wT, conv_bT, dt_bias, a_log, d_skip, ssd_norm_w = (g_(k) for k in
        ["conv_wT", "conv_bT", "dt_bias", "a_log", "d_skip", "ssd_norm_w"])
    tri_f_d, tri_b_d, sel_d, mrows_d = g_("tri_f_d"), g_("tri_b_d"), g_("sel_d"), g_("mrows_d")
    ident_f, r_identf, ident_b, r_identb = g_("ident_f"), g_("r_identf"), g_("ident_b"), g_("r_identb")
    with ExitStack() as st:
        bct, r_bct = sb(st, nc, "ss_bct", [128, 4, S], BF16)
        cs3T, r_cs3T = sb(st, nc, "ss_cs3T", [96, S], BF16)
        ncs3T, r_ncs3T = sb(st, nc, "ss_ncs3T", [128, S], BF16)
        sel, r_sel = sb(st, nc, "ss_sel", [128, 32, 128], BF16)
        pb.dma(ncs3T[96:128, :].rearrange("p (c l) -> p c l", l=128),
               mrows_d[:, :].unsqueeze(1).broadcast_to([32, 32, 128]), writes=[r_ncs3T])
        wst, r_wst = sb(st, nc, "ss_wst", [128, 32, 32], F32)
        ef, r_ef = sb(st, nc, "ss_ef", [128, 32, 32], F32)
        cdb, r_cdb = sb(st, nc, "ss_cdb", [128, 32, 32], F32)
        trif, r_trif = sb(st, nc, "ss_trif", [128, 128], F32)
        trib, r_trib = sb(st, nc, "ss_trib", [128, 128], F32)
        di, r_di = sb(st, nc, "ss_di", [128, 16, 128], BF16)
        nwb, r_nwb = sb(st, nc, "ss_nwb", [128, 1024], F32)
        dsk, r_dsk = sb(st, nc, "ss_dsk", [128, 16], F32)
        pb.dma(trif[:], tri_f_d[:, :], writes=[r_trif])
        pb.dma(trib[:], tri_b_d[:, :], writes=[r_trib])
        pb.dma(nwb[:], ssd_norm_w[L:L + 1, :].broadcast_to([128, 1024]), writes=[r_nwb])
        pb.dma(dsk[:], d_skip[L:L + 1, :].broadcast_to([128, 16]), writes=[r_dsk])

        for i in range(16):
            pb.op("dve", lambda e, i=i: e.tensor_scalar(out=di[:, i, :], in0=ident_f[:], scalar1=dsk[:, i:i + 1],
                                                        scalar2=None, op0=ALU.mult), reads=[r_identf, r_dsk], writes=[r_di])
        with ExitStack() as s1:
            selst, r_selst = sb(s1, nc, "ss_selst", [128, 32 * 128], F32)
            pb.dma(selst[:], sel_d[:, :], writes=[r_selst])
            pb.op("dve", lambda e: e.tensor_copy(out=sel[:].rearrange("p a b -> p (a b)"), in_=selst[:]),
                  reads=[r_selst], writes=[r_sel])
            cw, r_cw = sb(s1, nc, "ss_cw", [128, 12, 5], F32)
            cb, r_cb = sb(s1, nc, "ss_cb", [128, 12], F32)
            pb.dma(cw[:], conv_wT[L].rearrange("c p j -> p c j"), writes=[r_cw])
            pb.dma(cb[:], conv_bT[L], writes=[r_cb])
            xp_ring = Ring(s1, nc, "ss_xp", [128, S + 4], BF16, 2)
            for (t, r) in xp_ring.slots:
                pb.op("dve", lambda e, t=t: e.memset(t[:, 0:2], 0.0), writes=[r])
                pb.op("dve", lambda e, t=t: e.memset(t[:, S + 2:S + 4], 0.0), writes=[r])
            dg_ring = Ring(s1, nc, "ss_dg", [128, 5, 128], BF16, 2)
            cps_ring = Ring(s1, nc, "ss_cps", [128, 512], F32, 4, psum=True)
            tps_ring = Ring(s1, nc, "ss_tps", [128, 8, 128], BF16, 3, psum=True)
            xc_ring = Ring(s1, nc, "ss_xc", [128, 512], BF16, 4)
            xt_ring = Ring(s1, nc, "ss_xt", [128, 4, 128], BF16, 6)
            x3 = x_tok.rearrange("(t p) c -> p t c", p=128)
            b3 = b_tok.rearrange("(t p) c -> p t c", p=128)
            conv_pend = []

            def conv_tr(cc, blk, src, r_src):
                tp, r_tp = tps_ring.next()
                for i in range(4):
                    pb.op("pe", lambda e, tp=tp, src=src, i=i: e.transpose(
                        out=tp[:, i, :], in_=src[:, i * 128:(i + 1) * 128], identity=ident_b[:]),
                        reads=[r_src, r_identb], writes=[r_tp])
                xt, r_xt = xt_ring.next()
                pb.op("dve", lambda e, xt=xt, tp=tp: e.tensor_copy(out=xt[:], in_=tp[:, 0:4, :]),
                      reads=[r_tp], writes=[r_xt])
                if cc < 8:
                    pb.dma(x3[:, blk * 4:(blk + 1) * 4, cc * 128:(cc + 1) * 128], xt[:], reads=[r_xt], writes=[R["x_tok"]], q="sp")
                else:
                    pb.dma(b3[:, blk * 4:(blk + 1) * 4, (cc - 8) * 128:(cc - 7) * 128], xt[:], reads=[r_xt], writes=[R["b_tok"]], q="sp")

            for cc in range(12):
                xp, r_xp = xp_ring.next()
                pb.dma(xp[:, 2:S + 2], xbcT[cc], reads=[R["xbcT"]], writes=[r_xp])
                dg, r_dg = dg_ring.next()
                for j in range(5):
                    pb.op("dve", lambda e, dg=dg, j=j, cc=cc: e.tensor_scalar(
                        out=dg[:, j, :], in0=ident_f[:], scalar1=cw[:, cc, j:j + 1], scalar2=None, op0=ALU.mult),
                        reads=[r_identf, r_cw], writes=[r_dg])
                for blk in range(8):
                    ps, r_ps = cps_ring.next()
                    for j in range(5):
                        pb.op("pe", lambda e, ps=ps, dg=dg, xp=xp, j=j, blk=blk: e.matmul(
                            ps[:], lhsT=dg[:, j, :], rhs=xp[:, blk * 512 + j:blk * 512 + j + 512],
                            start=(j == 0), stop=(j == 4)), reads=[r_dg, r_xp], writes=[r_ps])
                    if cc >= 8:
                        dst = bct[:, cc - 8, blk * 512:(blk + 1) * 512]
                        pb.op("act", lambda e, dst=dst, ps=ps, cc=cc: e.activation(
                            out=dst, in_=ps[:], func=AF.Silu, bias=cb[:, cc:cc + 1]), reads=[r_ps, r_cb], writes=[r_bct])
                        src, r_src = dst, r_bct
                    else:
                        xc, r_xc = xc_ring.next()
                        pb.op("act", lambda e, xc=xc, ps=ps, cc=cc: e.activation(
                            out=xc[:], in_=ps[:], func=AF.Silu, bias=cb[:, cc:cc + 1]), reads=[r_ps, r_cb], writes=[r_xc])
                        src, r_src = xc[:], r_xc
                    if cc < 10:
                        conv_pend.append((cc, blk, src, r_src))
                    if len(conv_pend) > 1 or (cc == 11 and blk == 7 and conv_pend) or (cc >= 10 and conv_pend):
                        conv_tr(*conv_pend.pop(0))
        pb.barrier()
        if env['dbg'].get('ssd_stop') == 1:
            return
        with ExitStack() as s2:
            def t32(name):
                return sb(s2, nc, name, [128, 32, 32], F32)
            dtr, r_dtr = t32("ss_dtr")
            dt, r_dt = t32("ss_dt")
            adt, r_adt = t32("ss_adt")
            lnd, r_lnd = t32("ss_lnd")
            cs, r_cs = t32("ss_cs")
            ncs, r_ncs = t32("ss_ncs")
            tot, r_tot = t32("ss_tot")
            tmp, r_tmp = t32("ss_tmp")
            cs3, r_cs3 = sb(s2, nc, "ss_cs3", [128, 32, 96], BF16)
            ncs3, r_ncs3 = sb(s2, nc, "ss_ncs3", [128, 32, 96], BF16)
            dtb, r_dtb = sb(s2, nc, "ss_dtb", [128, 32], F32)
            alb, r_alb = sb(s2, nc, "ss_alb", [128, 32], F32)
            ones, r_ones = sb(s2, nc, "ss_ones", [128, 128], F32)
            ps_ring = Ring(s2, nc, "ss_pps", [128, 512], F32, 3, psum=True)
            tp_ring = Ring(s2, nc, "ss_ptp", [128, 8, 128], BF16, 2, psum=True)
            pb.op("dve", lambda e: e.memset(ones[:], 1.0), writes=[r_ones])
            d3 = dtraw.rearrange("(t p) h -> p t h", p=128)
            pb.dma(dtr[:, 0:16, :], d3[:, 0:16, :], reads=[R["dtraw"]], writes=[r_dtr])
            pb.dma(dtr[:, 16:32, :], d3[:, 16:32, :], reads=[R["dtraw"]], writes=[r_dtr])
            pb.dma(dtb[:], dt_bias[L:L + 1, :].broadcast_to([128, 32]), writes=[r_dtb])
            pb.dma(alb[:], a_log[L:L + 1, :].broadcast_to([128, 32]), writes=[r_alb])
            mid = lambda t: t[:].unsqueeze(1).broadcast_to([128, 32, 32])
            pb.op("act", lambda e: e.activation(out=alb[:], in_=alb[:], func=AF.Exp), reads=[r_alb], writes=[r_alb])
            pb.op("dve", lambda e: e.tensor_scalar(out=alb[:], in0=alb[:], scalar1=-1.0, scalar2=None, op0=ALU.mult),
                  reads=[r_alb], writes=[r_alb])
            pb.op("dve", lambda e: e.tensor_tensor(out=dtr[:], in0=dtr[:], in1=mid(dtb), op=ALU.add),
                  reads=[r_dtr, r_dtb], writes=[r_dtr])
            pb.op("act", lambda e: e.activation(out=tmp[:], in_=dtr[:], func=AF.Exp), reads=[r_dtr], writes=[r_tmp])
            pb.op("act", lambda e: e.activation(out=dt[:], in_=tmp[:], func=AF.Ln, bias=1.0), reads=[r_tmp], writes=[r_dt])
            pb.op("dve", lambda e: e.tensor_tensor(out=adt[:], in0=dt[:], in1=mid(alb), op=ALU.mult),
                  reads=[r_dt, r_alb], writes=[r_adt])
            pb.op("dve", lambda e: e.tensor_scalar(out=tmp[:], in0=dt[:], scalar1=1e-30, scalar2=None, op0=ALU.max),
                  reads=[r_dt], writes=[r_tmp])
            pb.op("act", lambda e: e.activation(out=lnd[:], in_=tmp[:], func=AF.Ln), reads=[r_tmp], writes=[r_lnd])
            for d_, tri, r_tri in ((0, trif, r_trif), (1, trib, r_trib)):
                ps, r_ps = ps_ring.next()
                pv = ps[:].rearrange("p (t h) -> p t h", h=16)
                pb.op("pe", lambda e, pv=pv, tri=tri, d_=d_: e.matmul(pv, lhsT=tri[:], rhs=adt[:, :, d_ * 16:(d_ + 1) * 16],
                                                                      start=True, stop=True), reads=[r_tri, r_adt], writes=[r_ps])
                pb.op("dve", lambda e, pv=pv, d_=d_: e.tensor_copy(out=cs[:, :, d_ * 16:(d_ + 1) * 16], in_=pv),
                      reads=[r_ps], writes=[r_cs])
            for hlf in range(2):
                ps, r_ps = ps_ring.next()
                pv = ps[:].rearrange("p (t h) -> p t h", h=32)
                pb.op("pe", lambda e, pv=pv, hlf=hlf: e.matmul(pv, lhsT=ones[:], rhs=adt[:, hlf * 16:(hlf + 1) * 16, :],
                                                               start=True, stop=True), reads=[r_ones, r_adt], writes=[r_ps])
                pb.op("dve", lambda e, pv=pv, hlf=hlf: e.tensor_copy(out=tot[:, hlf * 16:(hlf + 1) * 16, :], in_=pv),
                      reads=[r_ps], writes=[r_tot])
            pb.op("dve", lambda e: e.tensor_tensor(out=ncs[:], in0=lnd[:], in1=cs[:], op=ALU.subtract),
                  reads=[r_lnd, r_cs], writes=[r_ncs])
            pb.op("dve", lambda e: e.tensor_tensor(out=tmp[:], in0=tot[:], in1=ncs[:], op=ALU.add),
                  reads=[r_tot, r_ncs], writes=[r_tmp])
            pb.op("act", lambda e: e.activation(out=wst[:], in_=tmp[:], func=AF.Exp), reads=[r_tmp], writes=[r_wst])
            pb.op("act", lambda e: e.activation(out=ef[:], in_=cs[:], func=AF.Exp), reads=[r_cs], writes=[r_ef])
            pb.op("act", lambda e: e.activation(out=cdb[:], in_=tot[:], func=AF.Exp), reads=[r_tot], writes=[r_cdb])
            for src, r_src, dst, r_dst in ((cs, r_cs, cs3, r_cs3), (ncs, r_ncs, ncs3, r_ncs3)):
                pb.op("dve", lambda e, src=src, dst=dst: e.tensor_copy(out=dst[:, :, 0:32], in_=src[:]),
                      reads=[r_src], writes=[r_dst])
                pb.op("dve", lambda e, src=src, dst=dst: e.tensor_tensor(out=tmp[:], in0=src[:], in1=dst[:, :, 0:32], op=ALU.subtract),
                      reads=[r_src, r_dst], writes=[r_tmp])
                pb.op("dve", lambda e, dst=dst: e.tensor_copy(out=dst[:, :, 32:64], in_=tmp[:]),
                      reads=[r_tmp], writes=[r_dst])
                pb.op("dve", lambda e, dst=dst: e.tensor_tensor(out=tmp[:], in0=tmp[:], in1=dst[:, :, 32:64], op=ALU.subtract),
                      reads=[r_tmp, r_dst], writes=[r_tmp])
                pb.op("dve", lambda e, dst=dst: e.tensor_copy(out=dst[:, :, 64:96], in_=tmp[:]),
                      reads=[r_tmp], writes=[r_dst])
            for src, r_src, dstT, r_dstT in ((cs3, r_cs3, cs3T, r_cs3T), (ncs3, r_ncs3, ncs3T, r_ncs3T)):
                for t8 in range(4):
                    tp, r_tp = tp_ring.next()
                    for i in range(8):
                        t = t8 * 8 + i
                        pb.op("pe", lambda e, tp=tp, src=src, t=t, i=i: e.transpose(
                            out=tp[0:96, i, :], in_=src[:, t, :], identity=ident_b[:]), reads=[r_src, r_identb], writes=[r_tp])
                    pb.op("act", lambda e, dstT=dstT, tp=tp, t8=t8: e.activation(
                        out=dstT[0:96, t8 * 1024:(t8 + 1) * 1024].rearrange("p (a b) -> p a b", b=128), in_=tp[0:96, :, :],
                        func=AF.Copy), reads=[r_tp], writes=[r_dstT])
        pb.barrier()
        if env['dbg'].get('ssd_stop') == 2:
            return
        with ExitStack() as s3:
            x_ring = Ring(s3, nc, "ss_ax", [128, 1024], BF16, 2)
            b_ring = Ring(s3, nc, "ss_ab", [128, 256], BF16, 2)
            xw_ring = Ring(s3, nc, "ss_axw", [128, 512], BF16, 2)
            hbf_ring = Ring(s3, nc, "ss_ahbf", [128, 2, 512], BF16, 2)
            hb, r_hb = sb(s3, nc, "ss_ahb", [128, 2, 512], F32)
            sps_ring = Ring(s3, nc, "ss_asps", [128, 512], F32, 2, psum=True)
            pb.op("dve", lambda e: e.memset(hb[:], 0.0), writes=[r_hb])
            for c in range(31, -1, -1):
                hbf, r_hbf = hbf_ring.next()
                pb.op("act", lambda e, hbf=hbf: e.activation(out=hbf[:], in_=hb[:], func=AF.Copy), reads=[r_hb], writes=[r_hbf])
                pb.dma(hb_d[c], hbf[:].rearrange("p g f -> p (g f)"), reads=[r_hbf], writes=[R["hb_d"]])
                if c == 0:
                    break
                xt, r_xt = x_ring.next()
                bt, r_bt = b_ring.next()
                pb.dma(xt[:], x_tok[c * 128:(c + 1) * 128, :], reads=[R["x_tok"]], writes=[r_xt])
                pb.dma(bt[:], b_tok[c * 128:(c + 1) * 128, :], reads=[R["b_tok"]], writes=[r_bt])
                for g in range(2):
                    xw, r_xw = xw_ring.next()
                    hs = 16 + g * 8
                    pb.op("dve", lambda e, xw=xw, xt=xt, g=g, c=c, hs=hs: e.tensor_tensor(
                        out=xw[:].rearrange("p (k d) -> p k d", d=64), in0=xt[:, g * 512:(g + 1) * 512].rearrange("p (k d) -> p k d", d=64),
                        in1=bc_last(wst[:, c, hs:hs + 8], 64), op=ALU.mult), reads=[r_xt, r_wst], writes=[r_xw])
                    ps, r_ps = sps_ring.next()
                    pb.op("pe", lambda e, ps=ps, bt=bt, xw=xw, g=g: e.matmul(
                        ps[:], lhsT=bt[:, g * 128:(g + 1) * 128], rhs=xw[:], start=True, stop=True),
                        reads=[r_bt, r_xw], writes=[r_ps])
                    pb.op("dve", lambda e, g=g, c=c, hs=hs: e.tensor_tensor(
                        out=hb[:, g, :].rearrange("p (k d) -> p k d", d=64), in0=hb[:, g, :].rearrange("p (k d) -> p k d", d=64),
                        in1=bc_last(cdb[:, c, hs:hs + 8], 64), op=ALU.mult), reads=[r_hb, r_cdb], writes=[r_hb])
                    pb.op("dve", lambda e, g=g, ps=ps: e.tensor_tensor(out=hb[:, g, :], in0=hb[:, g, :], in1=ps[:], op=ALU.add),
                          reads=[r_hb, r_ps], writes=[r_hb])
        pb.barrier()
        if env['dbg'].get('ssd_stop') == 3:
            return
        with ExitStack() as s4:
            x_ring = Ring(s4, nc, "ss_bx", [128, 1024], BF16, 3)
            b_ring = Ring(s4, nc, "ss_bb", [128, 256], BF16, 3)
            z_ring = Ring(s4, nc, "ss_bz", [128, 1024], BF16, 3)
            h_ring = Ring(s4, nc, "ss_bh", [128, 1024], BF16, 3)
            gm_ring = Ring(s4, nc, "ss_gm", [128, 2, 2, 128], BF16, 2)
            lm_ring = Ring(s4, nc, "ss_lm", [128, 4, 128], BF16, 3)
            mt_ring = Ring(s4, nc, "ss_mt", [128, 32, 128], BF16, 2)
            xw_ring = Ring(s4, nc, "ss_bxw", [128, 512], BF16, 4)
            t1_ring = Ring(s4, nc, "ss_t1", [128, 512], BF16, 4)
            t2_ring = Ring(s4, nc, "ss_t2", [128, 512], BF16, 4)
            yz_ring = Ring(s4, nc, "ss_yz", [128, 512], F32, 4)
            sq_ring = Ring(s4, nc, "ss_sq", [128, 512], BF16, 3)
            ss_ring = Ring(s4, nc, "ss_ss", [128, 2], F32, 6)
            o_ring = Ring(s4, nc, "ss_o", [128, 1024], BF16, 2)
            hf, r_hf = sb(s4, nc, "ss_hf", [128, 2, 512], F32)
            hfb, r_hfb = sb(s4, nc, "ss_hfb", [128, 2, 512], BF16)
            gps_ring = Ring(s4, nc, "ss_gps", [128, 512], F32, 1, psum=True)
            dps_ring = Ring(s4, nc, "ss_dps", [128, 512], F32, 2, psum=True)
            yd_ring = Ring(s4, nc, "ss_yd", [128, 512], F32, 2, psum=True)
            yo_ring = Ring(s4, nc, "ss_yo", [128, 512], F32, 1, psum=True)
            sps_ring = Ring(s4, nc, "ss_bsps", [128, 512], F32, 2, psum=True)
            pb.op("dve", lambda e: e.memset(hf[:], 0.0), writes=[r_hf])
            pb.op("dve", lambda e: e.memset(hfb[:], 0.0), writes=[r_hfb])
            def sw_load(c):
                cs_ = slice(c * 128, (c + 1) * 128)
                xt, r_xt = x_ring.next()
                bt, r_bt = b_ring.next()
                zt, r_zt = z_ring.next()
                ht, r_ht = h_ring.next()
                pb.dma(xt[:], x_tok[cs_, :], reads=[R["x_tok"]], writes=[r_xt])
                pb.dma(bt[:], b_tok[cs_, :], reads=[R["b_tok"]], writes=[r_bt])
                pb.dma(zt[:], zs[cs_, :], reads=[R["zs"]], writes=[r_zt])
                pb.dma(ht[:], hb_d[c], reads=[R["hb_d"]], writes=[r_ht])
                return dict(xt=xt, r_xt=r_xt, bt=bt, r_bt=r_bt, zt=zt, r_zt=r_zt, ht=ht, r_ht=r_ht)

            def sw_dphase(c):
                cs_ = slice(c * 128, (c + 1) * 128)
                gps, r_gps = gps_ring.next()
                gm, r_gm = gm_ring.next()
                for g in range(2):
                    pb.op("pe", lambda e, gps=gps, g=g, cs_=cs_: e.matmul(
                        gps[:, g * 128:(g + 1) * 128], lhsT=bct[:, g, cs_], rhs=bct[:, 2 + g, cs_], start=True, stop=True),
                        reads=[r_bct], writes=[r_gps])
                for g in range(2):
                    pb.op("dve", lambda e, gm=gm, gps=gps, g=g: e.tensor_tensor(
                        out=gm[:, 0, g, :], in0=gps[:, g * 128:(g + 1) * 128], in1=trif[:], op=ALU.mult),
                        reads=[r_gps, r_trif], writes=[r_gm])
                    pb.op("dve", lambda e, gm=gm, gps=gps, g=g: e.tensor_tensor(
                        out=gm[:, 1, g, :], in0=gps[:, g * 128:(g + 1) * 128], in1=trib[:], op=ALU.mult),
                        reads=[r_gps, r_trib], writes=[r_gm])
                mt, r_mt = mt_ring.next()
                for q in range(8):
                    d_, g = q // 4, (q // 2) % 2
                    dps, r_dps = dps_ring.next()
                    dv = dps[:].rearrange("p (j l) -> p j l", l=128)
                    pb.op("pe", lambda e, dv=dv, q=q, cs_=cs_: e.matmul(
                        dv, lhsT=ncs3T[:, cs_], rhs=sel[:, 4 * q:4 * q + 4, :], start=True, stop=False),
                        reads=[r_sel, r_ncs3T], writes=[r_dps])
                    for j in range(4):
                        hd = 4 * q + j
                        pb.op("pe", lambda e, dv=dv, j=j, hd=hd, cs_=cs_: e.matmul(
                            dv[:, j, :], lhsT=sel[0:96, hd, :], rhs=cs3T[:, cs_], start=False, stop=(j == 3)),
                            reads=[r_sel, r_cs3T], writes=[r_dps])
                    lm, r_lm = lm_ring.next()
                    pb.op("act", lambda e, lm=lm, dv=dv: e.activation(out=lm[:], in_=dv, func=AF.Exp), reads=[r_dps], writes=[r_lm])
                    pb.op("dve", lambda e, mt=mt, lm=lm, gm=gm, q=q, d_=d_, g=g: e.scalar_tensor_tensor(
                        out=mt[:, 4 * q:4 * q + 4, :], in0=lm[:], scalar=BIGF,
                        in1=gm[:, d_, g, :].unsqueeze(1).broadcast_to([128, 4, 128]), op0=ALU.min, op1=ALU.mult),
                        reads=[r_lm, r_gm], writes=[r_mt])
                return mt, r_mt

            def sw_ystate(c, ld, mt, r_mt):
                cs_ = slice(c * 128, (c + 1) * 128)
                xt, r_xt, zt, r_zt, ht, r_ht = ld["xt"], ld["r_xt"], ld["zt"], ld["r_zt"], ld["ht"], ld["r_ht"]
                bt, r_bt = ld["bt"], ld["r_bt"]
                o, r_o = o_ring.next()
                tt = []
                for g in range(2):
                    yf, r_yf = yo_ring.next()
                    pb.op("pe", lambda e, yf=yf, g=g, cs_=cs_: e.matmul(
                        yf[:], lhsT=bct[:, 2 + g, cs_], rhs=hfb[:, g, :], start=True, stop=True), reads=[r_bct, r_hfb], writes=[r_yf])
                    t1, r_t1 = t1_ring.next()
                    pb.op("dve", lambda e, t1=t1, yf=yf, g=g, c=c: e.tensor_tensor(
                        out=t1[:].rearrange("p (k d) -> p k d", d=64), in0=yf[:].rearrange("p (k d) -> p k d", d=64),
                        in1=bc_last(ef[:, c, g * 8:g * 8 + 8], 64), op=ALU.mult), reads=[r_yf, r_ef], writes=[r_t1])
                    yb, r_yb = yo_ring.next()
                    pb.op("pe", lambda e, yb=yb, g=g, cs_=cs_, ht=ht: e.matmul(
                        yb[:], lhsT=bct[:, 2 + g, cs_], rhs=ht[:, g * 512:(g + 1) * 512], start=True, stop=True),
                        reads=[r_bct, r_ht], writes=[r_yb])
                    t2, r_t2 = t2_ring.next()
                    pb.op("dve", lambda e, t2=t2, yb=yb, g=g, c=c: e.tensor_tensor(
                        out=t2[:].rearrange("p (k d) -> p k d", d=64), in0=yb[:].rearrange("p (k d) -> p k d", d=64),
                        in1=bc_last(ef[:, c, 16 + g * 8:16 + g * 8 + 8], 64), op=ALU.mult), reads=[r_yb, r_ef], writes=[r_t2])
                    tt.append((t1, r_t1, t2, r_t2))
                yds = []
                for g in range(2):
                    t1, r_t1, t2, r_t2 = tt[g]
                    yd, r_yd = yd_ring.next()
                    ydv = yd[:].rearrange("p (k d) -> p k d", d=64)
                    pb.op("pe", lambda e, yd=yd, t1=t1: e.matmul(yd[:], lhsT=ident_b[:], rhs=t1[:], start=True, stop=False),
                          reads=[r_identb, r_t1], writes=[r_yd])
                    pb.op("pe", lambda e, yd=yd, t2=t2: e.matmul(yd[:], lhsT=ident_b[:], rhs=t2[:], start=False, stop=False),
                          reads=[r_identb, r_t2], writes=[r_yd])
                    for k in range(8):
                        hi = g * 8 + k
                        xs_ = xt[:, hi * 64:(hi + 1) * 64]
                        pb.op("pe", lambda e, ydv=ydv, k=k, hi=hi, xs_=xs_, mt=mt: e.matmul(
                            ydv[:, k, :], lhsT=mt[:, hi, :], rhs=xs_, start=False, stop=False), reads=[r_mt, r_xt], writes=[r_yd])
                        pb.op("pe", lambda e, ydv=ydv, k=k, hi=hi, xs_=xs_, mt=mt: e.matmul(
                            ydv[:, k, :], lhsT=mt[:, 16 + hi, :], rhs=xs_, start=False, stop=False), reads=[r_mt, r_xt], writes=[r_yd])
                        pb.op("pe", lambda e, ydv=ydv, k=k, hi=hi, xs_=xs_: e.matmul(
                            ydv[:, k, :], lhsT=di[:, hi, :], rhs=xs_, start=False, stop=(k == 7)), reads=[r_di, r_xt], writes=[r_yd])
                    yds.append((yd, r_yd))
                xws = []
                if c < 31:
                    for g in range(2):
                        xw, r_xw = xw_ring.next()
                        hs = g * 8
                        pb.op("dve", lambda e, xw=xw, xt=xt, g=g, c=c, hs=hs: e.tensor_tensor(
                            out=xw[:].rearrange("p (k d) -> p k d", d=64), in0=xt[:, g * 512:(g + 1) * 512].rearrange("p (k d) -> p k d", d=64),
                            in1=bc_last(wst[:, c, hs:hs + 8], 64), op=ALU.mult), reads=[r_xt, r_wst], writes=[r_xw])
                        ps, r_ps = sps_ring.next()
                        pb.op("pe", lambda e, ps=ps, bt=bt, xw=xw, g=g: e.matmul(
                            ps[:], lhsT=bt[:, g * 128:(g + 1) * 128], rhs=xw[:], start=True, stop=True),
                            reads=[r_bt, r_xw], writes=[r_ps])
                        xws.append((ps, r_ps))
                yzs = []
                for g in range(2):
                    yd, r_yd = yds[g]
                    yz, r_yz = yz_ring.next()
                    pb.op("dve", lambda e, yz=yz, yd=yd, zt=zt, g=g: e.tensor_tensor(
                        out=yz[:], in0=yd[:], in1=zt[:, g * 512:(g + 1) * 512], op=ALU.mult), reads=[r_yd, r_zt], writes=[r_yz])
                    sq, r_sq = sq_ring.next()
                    ss, r_ss = ss_ring.next()
                    pb.op("act", lambda e, sq=sq, yz=yz, ss=ss: e.activation(
                        out=sq[:], in_=yz[:], func=AF.Square, accum_out=ss[:, 0:1]), reads=[r_yz], writes=[r_sq, r_ss])
                    yzs.append((yz, r_yz, ss, r_ss))
                if c < 31:
                    for g in range(2):
                        hs = g * 8
                        pb.op("dve", lambda e, g=g, c=c, hs=hs: e.tensor_tensor(
                            out=hf[:, g, :].rearrange("p (k d) -> p k d", d=64), in0=hf[:, g, :].rearrange("p (k d) -> p k d", d=64),
                            in1=bc_last(cdb[:, c, hs:hs + 8], 64), op=ALU.mult), reads=[r_hf, r_cdb], writes=[r_hf])
                for g in range(2):
                    yz, r_yz, ss, r_ss = yzs[g]
                    pb.op("dve", lambda e, ss=ss: e.tensor_scalar(
                        out=ss[:, 1:2], in0=ss[:, 0:1], scalar1=1.0 / 512, scalar2=EPS, op0=ALU.mult, op1=ALU.add),
                        reads=[r_ss], writes=[r_ss])
                    pb.op("act", lambda e, ss=ss: e.activation(out=ss[:, 1:2], in_=ss[:, 1:2], func=AF.Sqrt),
                          reads=[r_ss], writes=[r_ss])
                if c < 31:
                    for g in range(2):
                        ps, r_ps = xws[g]
                        pb.op("dve", lambda e, g=g, ps=ps: e.tensor_tensor(out=hf[:, g, :], in0=hf[:, g, :], in1=ps[:], op=ALU.add),
                              reads=[r_hf, r_ps], writes=[r_hf])
                        pb.op("act", lambda e, g=g: e.activation(out=hfb[:, g, :], in_=hf[:, g, :], func=AF.Copy),
                              reads=[r_hf], writes=[r_hfb])
                for g in range(2):
                    yz, r_yz, ss, r_ss = yzs[g]
                    pb.op("dve", lambda e, ss=ss: e.reciprocal(out=ss[:, 1:2], in_=ss[:, 1:2]), reads=[r_ss], writes=[r_ss])
                    pb.op("dve", lambda e, o=o, yz=yz, ss=ss, g=g: e.scalar_tensor_tensor(
                        out=o[:, g * 512:(g + 1) * 512], in0=yz[:], scalar=ss[:, 1:2], in1=nwb[:, g * 512:(g + 1) * 512],
                        op0=ALU.mult, op1=ALU.mult), reads=[r_yz, r_ss, r_nwb], writes=[r_o])
                pb.dma(mixed[cs_, 512:1536], o[:], reads=[r_o], writes=[R["mixed"]])

            ld_cur = sw_load(0)
            mt_cur = sw_dphase(0)
            for c in range(32):
                if c + 1 < 32:
                    ld_nxt = sw_load(c + 1)
                    mt_nxt = sw_dphase(c + 1)
                sw_ystate(c, ld_cur, *mt_cur)
                if c < 31:
                    ld_cur, mt_cur = ld_nxt, mt_nxt


def phase_fnet(nc, pb, L, R, env):
    g_ = lambda k: env[k]
    u_tok, gf, mixed, zbuf = g_("u_tok"), g_("gf"), g_("mixed"), g_("zbuf")
    w_fourier, fcs_d, ccs_d, w3_d = g_("w_fourier"), g_("fcs_d"), g_("ccs_d"), g_("w3_d")
    KAP = 1.0 / float(np.sqrt(4096.0 * 128.0))
    with ExitStack() as st:
        fcs, r_fcs = sb(st, nc, "fn_fcs", [128, 32, 256], BF16)
        w3, r_w3 = sb(st, nc, "fn_w3", [128, 64], BF16)
        pb.dma(fcs[:, 0:16, :], fcs_d[:, 0:16, :], writes=[r_fcs])
        pb.dma(fcs[:, 16:32, :], fcs_d[:, 16:32, :], writes=[r_fcs])
        pb.dma(w3[:], w3_d[:, :], writes=[r_w3])
        with ExitStack() as s1:
            ccs, r_ccs = sb(s1, nc, "fn_ccs", [128, 2, 128], F32)
            wf, r_wf = sb(s1, nc, "fn_wf", [128, 4, 128], F32)
            w12, r_w12 = sb(s1, nc, "fn_w12", [128, 4, 2, 256], BF16)
            ut, r_ut = sb(s1, nc, "fn_u", [128, 32, 512], BF16)
            pb.dma(ccs[:], ccs_d[:, :, :], writes=[r_ccs])
            pb.dma(wf[:], w_fourier[L].rearrange("g c d -> c g d"), writes=[r_wf])
            pb.dma(ut[:, 0:16, :], u_tok.rearrange("(a b) c -> a b c", b=32)[:, 0:16, :], reads=[R["u_tok"]], writes=[r_ut])
            pb.dma(ut[:, 16:32, :], u_tok.rearrange("(a b) c -> a b c", b=32)[:, 16:32, :], reads=[R["u_tok"]], writes=[r_ut])
            ps_ring = Ring(s1, nc, "fn_ps", [128, 512], F32, 4, psum=True)
            yt_ring = Ring(s1, nc, "fn_yt", [128, 32, 256], BF16, 2)
            zs_ring = Ring(s1, nc, "fn_zs", [128, 8, 256], BF16, 2)
            for g in range(4):
                ps, r_ps = ps_ring.next()
                for ri in range(2):
                    pb.op("pe", lambda e, ps=ps, ri=ri, g=g: e.matmul(
                        ps[:, ri * 128:(ri + 1) * 128], lhsT=ccs[:, ri, :], rhs=wf[:, g, :], start=True, stop=True),
                        reads=[r_ccs, r_wf], writes=[r_ps])
                pb.op("act", lambda e, ps=ps, g=g: e.activation(out=w12[:, g, 0, :], in_=ps[:, 0:256], func=AF.Copy, scale=KAP),
                      reads=[r_ps], writes=[r_w12])
                pb.op("act", lambda e, ps=ps, g=g: e.activation(out=w12[:, g, 1, 0:128], in_=ps[:, 128:256], func=AF.Copy, scale=-KAP),
                      reads=[r_ps], writes=[r_w12])
                pb.op("act", lambda e, ps=ps, g=g: e.activation(out=w12[:, g, 1, 128:256], in_=ps[:, 0:128], func=AF.Copy, scale=KAP),
                      reads=[r_ps], writes=[r_w12])
            ne = 0
            prev_yt_readers = {}
            for g in range(4):
                yt, _r_yt_unused = yt_ring.next()
                r_ytl = [Res("yt%d" % b2_) for b2_ in range(16)]
                if g >= 2:
                    for b2_ in range(16):
                        r_ytl[b2_].r = dict(prev_yt_readers[g - 2])
                for b2 in range(16):
                    r_yt = r_ytl[b2]
                    ps, r_ps = ps_ring.next()
                    for i in range(2):
                        b = b2 * 2 + i
                        pb.op("pe", lambda e, ps=ps, i=i, b=b, g=g: e.matmul(
                            ps[:, i * 256:(i + 1) * 256], lhsT=ut[:, b, g * 128:(g + 1) * 128], rhs=fcs[:, b, :],
                            start=True, stop=True), reads=[r_ut, r_fcs], writes=[r_ps])
                    ne += 1
                    dst = yt[:, b2 * 2:b2 * 2 + 2, :].rearrange("p a b -> p (a b)")
                    if ne % 2 == 0:
                        pb.op("act", lambda e, dst=dst, ps=ps: e.activation(out=dst, in_=ps[:], func=AF.Copy), reads=[r_ps], writes=[r_yt])
                    else:
                        pb.op("dve", lambda e, dst=dst, ps=ps: e.tensor_copy(out=dst, in_=ps[:]), reads=[r_ps], writes=[r_yt])
                for b8 in range(4):
                    zst, r_zst = zs_ring.next()
                    for b2 in range(4):
                        ps, r_ps = ps_ring.next()
                        for i in range(2):
                            b = b8 * 8 + b2 * 2 + i
                            r_yt = r_ytl[b // 2]
                            pb.op("pe", lambda e, ps=ps, i=i, b=b, g=g, yt=yt: e.matmul(
                                ps[:, i * 256:(i + 1) * 256], lhsT=yt[:, b, 0:128], rhs=w12[:, g, 0, :], start=True, stop=False),
                                reads=[r_yt, r_w12], writes=[r_ps])
                            pb.op("pe", lambda e, ps=ps, i=i, b=b, g=g, yt=yt: e.matmul(
                                ps[:, i * 256:(i + 1) * 256], lhsT=yt[:, b, 128:256], rhs=w12[:, g, 1, :], start=False, stop=True),
                                reads=[r_yt, r_w12], writes=[r_ps])
                        ne += 1
                        dst = zst[:, b2 * 2:b2 * 2 + 2, :].rearrange("p a b -> p (a b)")
                        if ne % 2 == 0:
                            pb.op("act", lambda e, dst=dst, ps=ps: e.activation(out=dst, in_=ps[:], func=AF.Copy), reads=[r_ps], writes=[r_zst])
                        else:
                            pb.op("dve", lambda e, dst=dst, ps=ps: e.tensor_copy(out=dst, in_=ps[:]), reads=[r_ps], writes=[r_zst])
                    for ri in range(2):
                        pb.dma(zbuf[:, b8 * 8:(b8 + 1) * 8, ri, g, :], zst[:, :, ri * 128:(ri + 1) * 128],
                               reads=[r_zst], writes=[R["zbuf"]])
                mrg = {}
                for r_ in r_ytl:
                    for k_, v_ in r_.r.items():
                        if mrg.get(k_, 0) < v_:
                            mrg[k_] = v_
                prev_yt_readers[g] = mrg
        pb.barrier()
        with ExitStack() as s2:
            zr_ring = Ring(s2, nc, "fn_zr", [128, 8, 512], BF16, 2)
            gt_ring = Ring(s2, nc, "fn_gt", [64, 8, 512], BF16, 2)
            ot_ring = Ring(s2, nc, "fn_ot", [64, 8, 512], BF16, 2)
            ps_ring = Ring(s2, nc, "fn_ps2", [128, 512], F32, 4, psum=True)
            gf3 = gf.rearrange("(bp r) c -> bp r c", r=128)
            mx3 = mixed.rearrange("(bp r) c -> bp r c", r=128)
            for ab in range(8):
                zr, r_zr = zr_ring.next()
                gt, r_gt = gt_ring.next()
                ot, r_ot = ot_ring.next()
                for a2 in range(2):
                    a0 = a2 * 64 + ab * 8
                    for ri in range(2):
                        p0 = a2 * 64 + ri * 32
                        pb.dma(zr[p0:p0 + 32, :, :],
                               zbuf[a0:a0 + 8, :, ri, :, :].rearrange("a b g d -> b a (g d)"),
                               reads=[R["zbuf"]], writes=[r_zr])
                    pb.dma(gt[a2 * 32:(a2 + 1) * 32, :, :], gf3[:, a0:a0 + 8, :], reads=[R["gf"]], writes=[r_gt])
                for ai in range(8):
                    ps, r_ps = ps_ring.next()
                    pb.op("pe", lambda e, ps=ps, zr=zr, ai=ai: e.matmul(
                        ps[0:64, :], lhsT=w3[:], rhs=zr[:, ai, :], start=True, stop=True), reads=[r_w3, r_zr], writes=[r_ps])
                    pb.op("dve", lambda e, ot=ot, ps=ps, gt=gt, ai=ai: e.tensor_tensor(
                        out=ot[:, ai, :], in0=ps[0:64, :], in1=gt[:, ai, :], op=ALU.mult), reads=[r_ps, r_gt], writes=[r_ot])
                for a2 in range(2):
                    a0 = a2 * 64 + ab * 8
                    pb.dma(mx3[:, a0:a0 + 8, 1536:2048], ot[a2 * 32:(a2 + 1) * 32, :, :], reads=[r_ot], writes=[R["mixed"]])


def _make_rpbT(na_rpb):
    nl = na_rpb.shape[0]
    kc = np.arange(64)[:, None]
    qc = np.arange(64)[None, :]
    c0 = np.clip(qc - 8, 0, 48)
    valid = (kc >= c0) & (kc < c0 + 16)
    idx = np.clip(kc - qc + 15, 0, 30)
    g = na_rpb[:, :, :, idx]
    g = np.where(valid[None, None, None], g, np.float32(-30000.0)).astype(np.float32)
    return np.ascontiguousarray(g.transpose(0, 1, 3, 2, 4))


def _constants():
    d = {"ident": np.eye(128, dtype=np.float32)}
    m = np.arange(128)[:, None]
    s_ = np.arange(128)[None, :]
    d["tri_f"] = (m <= s_).astype(np.float32)
    d["tri_b"] = (m >= s_).astype(np.float32)
    sel = np.zeros((128, 32, 128), np.float32)
    for k in range(96):
        sel[k, k % 32, :] = 1.0
    l_ = np.arange(128)
    mrows = np.zeros((32, 128), np.float32)
    for k in range(15):
        tau = 8 * (k + 1)
        sel[96 + k, 0:16, :] = np.where(l_ < tau, -30000.0, 0.0)[None, :]
        sel[112 + k, 16:32, :] = np.where(l_ >= tau, -30000.0, 0.0)[None, :]
        mrows[k, :] = (l_ >= tau).astype(np.float32)
        mrows[16 + k, :] = (l_ < tau).astype(np.float32)
    d["sel"] = sel.reshape(128, 32 * 128)
    d["mrows"] = mrows.astype(ml_dtypes.bfloat16)
    a = np.arange(128)[:, None, None].astype(np.float64)
    b = np.arange(32)[None, :, None].astype(np.float64)
    ap = np.arange(128)[None, None, :].astype(np.float64)
    ang = 2.0 * np.pi * ((ap * (32.0 * a + b)) % 4096.0) / 4096.0
    fcs = np.stack([np.cos(ang), np.sin(ang)], axis=2)
    d["fcs"] = fcs.reshape(128, 32, 256).astype(np.float32).astype(ml_dtypes.bfloat16)
    c = np.arange(128)[:, None].astype(np.float64)
    cp = np.arange(128)[None, :].astype(np.float64)
    angc = 2.0 * np.pi * ((c * cp) % 128.0) / 128.0
    d["ccs"] = np.ascontiguousarray(np.stack([np.cos(angc), np.sin(angc)], axis=1).astype(np.float32))
    w3 = np.zeros((2, 2, 32, 2, 32), np.float64)
    bb = np.arange(32)[:, None] * np.arange(32)[None, :]
    ang3 = 2.0 * np.pi * (bb % 32) / 32.0
    for a2 in range(2):
        w3[a2, 0, :, a2, :] = np.cos(ang3)
        w3[a2, 1, :, a2, :] = -np.sin(ang3)
    d["w3"] = w3.reshape(128, 64).astype(np.float32).astype(ml_dtypes.bfloat16)
    return d


def host_inputs(x, norm_w, w_in, na_rpb, conv_w, conv_b, dt_bias, a_log, d_skip, ssd_norm_w,
                w_fourier, w_out, final_norm_w):
    f = lambda a: np.ascontiguousarray(np.asarray(a, dtype=np.float32))
    nl = norm_w.shape[0]
    sh = {
        "norm_w": f(np.asarray(norm_w).reshape(nl, 8, 128).transpose(0, 2, 1)),
        "w_in": f(w_in), "w_out": f(w_out),
        "final_norm_w": f(np.asarray(final_norm_w)[None]),
        "rpbT": _make_rpbT(f(na_rpb)),
        "conv_wT": f(np.asarray(conv_w).reshape(nl, 5, 12, 128).transpose(0, 2, 3, 1)),
        "conv_bT": f(np.asarray(conv_b).reshape(nl, 12, 128).transpose(0, 2, 1)),
        "dt_bias": f(np.asarray(dt_bias).reshape(nl, 32)),
        "a_log": f(np.asarray(a_log).reshape(nl, 32)),
        "d_skip": f(d_skip), "ssd_norm_w": f(ssd_norm_w), "w_fourier": f(w_fourier),
    }
    sh.update(_constants())
    return sh


_NC_CACHE = {}


def kernel(x, norm_w, w_in, na_rpb, conv_w, conv_b, dt_bias, a_log, d_skip, ssd_norm_w,
           w_fourier, w_out, final_norm_w):
    x = np.asarray(x, dtype=np.float32)
    shared = host_inputs(x, norm_w, w_in, na_rpb, conv_w, conv_b, dt_bias, a_log, d_skip, ssd_norm_w,
                         w_fourier, w_out, final_norm_w)
    if "nc" not in _NC_CACHE:
        _NC_CACHE["nc"] = build_program()
    nc = _NC_CACHE["nc"]
    n = x.shape[0]
    in_maps = []
    for b in range(n):
        m = dict(shared)
        m["x"] = np.ascontiguousarray(x[b])
        in_maps.append(m)
    res = run_bass_kernel_spmd(nc, in_maps, core_ids=list(range(n)))
    return np.stack([np.asarray(r["out"], dtype=np.float32) for r in res.results], axis=0)
```
